# Optimizing a Trainium2 kernel written in Bass

```python
import math
import jax, jax.numpy as jnp
from jax import lax
import numpy as np

D_MODEL = 1024
BATCH = 2
SEQ = 16384
DEPTH = 4

N_MIXERS = 3
SB_HEAD_DIM = 128
SB_HEADS = D_MODEL // SB_HEAD_DIM
DIFF_HALF_DIM = 128
DIFF_HEADS = D_MODEL // (2 * DIFF_HALF_DIM)
MEM_HEAD_DIM = 64
MEM_HEADS = 4
MEM_LEN = 256
MEM_WIDTH = MEM_HEADS * MEM_HEAD_DIM
MIX_WIDTH = D_MODEL
IN_WIDTH = 3 * MIX_WIDTH + MEM_WIDTH
OUT_IN_WIDTH = MIX_WIDTH + MEM_WIDTH
CONV_WIDTH = 3
D_FF = -(-8 * D_MODEL // (3 * 256)) * 256
BLOCK_Q = 128
N_DIFF = (DEPTH + 1) // 3
N_CONV = DEPTH // 3
NORM_EPS = 1e-6
HEAD_NORM_EPS = 1e-5
LAMBDA_STD = 0.1

kernel_name = "hybrid_sb_diff_conv_mem_trunk"


def _rmsnorm(x, g, eps=NORM_EPS):
    xf = x.astype(jnp.float32)
    y = xf * lax.rsqrt(jnp.mean(xf * xf, axis=-1, keepdims=True) + eps)
    return (y * g.astype(jnp.float32)).astype(x.dtype)


def _heads(t, n_heads, d):
    b, s, _ = t.shape
    return t.reshape(b, s, n_heads, d).transpose(0, 2, 1, 3)


def _rev_cumsum(u):
    shp = u.shape
    nc = shp[-1] // BLOCK_Q
    uc = u.reshape(shp[:-1] + (nc, BLOCK_Q))
    r = jnp.arange(BLOCK_Q)
    tri = (r[:, None] >= r[None, :]).astype(u.dtype)
    within = jnp.einsum('...cj,js->...cs', uc, tri, precision=lax.Precision.HIGHEST)
    c = jnp.arange(nc)
    later = (c[:, None] > c[None, :]).astype(u.dtype)
    suffix = jnp.einsum('...k,kc->...c', within[..., 0], later,
                        precision=lax.Precision.HIGHEST)
    return (within + suffix[..., None]).reshape(shp)


def _stick_breaking(cols):
    b, seq, _ = cols.shape
    q = _heads(cols[..., :MIX_WIDTH], SB_HEADS, SB_HEAD_DIM) * (SB_HEAD_DIM ** -0.5)
    k = _heads(cols[..., MIX_WIDTH:2 * MIX_WIDTH], SB_HEADS, SB_HEAD_DIM)
    v = _heads(cols[..., 2 * MIX_WIDTH:], SB_HEADS, SB_HEAD_DIM)
    outs = []
    for i in range(seq // BLOCK_Q):
        end = (i + 1) * BLOCK_Q
        qb = q[:, :, i * BLOCK_Q:end]
        z = jnp.einsum('bhqd,bhkd->bhqk', qb, k[:, :, :end]).astype(jnp.float32)
        qpos = i * BLOCK_Q + jnp.arange(BLOCK_Q)
        mask = jnp.arange(end)[None, :] < qpos[:, None]
        u = jnp.where(mask, jax.nn.softplus(z), 0.0)
        a = jnp.where(mask, jnp.exp(z - _rev_cumsum(u)), 0.0)
        outs.append(jnp.einsum('bhqk,bhkd->bhqd', a.astype(v.dtype), v[:, :, :end]))
    o = jnp.concatenate(outs, axis=2)
    return o.transpose(0, 2, 1, 3).reshape(b, seq, MIX_WIDTH)


def _diff_attention(cols, lam_q1, lam_k1, lam_q2, lam_k2, g_head, layer_idx):
    b, seq, _ = cols.shape
    scale = DIFF_HALF_DIM ** -0.5
    qh = cols[..., :MIX_WIDTH].reshape(b, seq, DIFF_HEADS, 2, DIFF_HALF_DIM) * scale
    kh = cols[..., MIX_WIDTH:2 * MIX_WIDTH].reshape(b, seq, DIFF_HEADS, 2, DIFF_HALF_DIM)
    q1 = qh[..., 0, :].transpose(0, 2, 1, 3)
    q2 = qh[..., 1, :].transpose(0, 2, 1, 3)
    k1 = kh[..., 0, :].transpose(0, 2, 1, 3)
    k2 = kh[..., 1, :].transpose(0, 2, 1, 3)
    v = _heads(cols[..., 2 * MIX_WIDTH:], DIFF_HEADS, 2 * DIFF_HALF_DIM)

    lambda_init = 0.8 - 0.6 * math.exp(-0.3 * layer_idx)
    f32 = jnp.float32
    lam = (jnp.exp(jnp.sum(lam_q1.astype(f32) * lam_k1.astype(f32)))
           - jnp.exp(jnp.sum(lam_q2.astype(f32) * lam_k2.astype(f32))) + lambda_init)
    slopes = 2.0 ** (-8.0 * jnp.arange(1, DIFF_HEADS + 1, dtype=f32) / DIFF_HEADS)

    outs = []
    for i in range(seq // BLOCK_Q):
        end = (i + 1) * BLOCK_Q
        qpos = i * BLOCK_Q + jnp.arange(BLOCK_Q)
        dist = (qpos[:, None] - jnp.arange(end)[None, :]).astype(f32)
        mask = dist >= 0.0
        bias = -slopes[:, None, None] * dist

        def probs(qq, kk):
            s = jnp.einsum('bhqd,bhkd->bhqk', qq[:, :, i * BLOCK_Q:end],
                           kk[:, :, :end]).astype(f32) + bias
            return jax.nn.softmax(jnp.where(mask, s, -jnp.inf), axis=-1)

        a = probs(q1, k1) - lam * probs(q2, k2)
        outs.append(jnp.einsum('bhqk,bhkd->bhqd', a.astype(v.dtype), v[:, :, :end]))
    o = jnp.concatenate(outs, axis=2).transpose(0, 2, 1, 3)
    o = _rmsnorm(o, g_head, HEAD_NORM_EPS) * (1.0 - lambda_init)
    return o.reshape(b, seq, MIX_WIDTH)


def _short_conv(cols, conv_w):
    gate_b = cols[..., :MIX_WIDTH]
    gate_c = cols[..., MIX_WIDTH:2 * MIX_WIDTH]
    h = cols[..., 2 * MIX_WIDTH:]
    u = gate_c * h
    y = lax.conv_general_dilated(
        u, conv_w[:, None, :].astype(u.dtype),
        window_strides=(1,), padding=[(CONV_WIDTH - 1, 0)],
        dimension_numbers=('NWC', 'WIO', 'NWC'),
        feature_group_count=MIX_WIDTH)
    return gate_b * y


def _memory_attention(mq, mem_n, w_kv):
    b, seq, _ = mq.shape
    kv = mem_n @ w_kv
    km = kv[..., :MEM_WIDTH].reshape(b, MEM_LEN, MEM_HEADS, MEM_HEAD_DIM)
    vm = kv[..., MEM_WIDTH:].reshape(b, MEM_LEN, MEM_HEADS, MEM_HEAD_DIM)
    q = mq.reshape(b, seq, MEM_HEADS, MEM_HEAD_DIM)
    s = jnp.einsum('bshd,bmhd->bhsm', q, km).astype(jnp.float32) * MEM_HEAD_DIM ** -0.5
    p = jax.nn.softmax(s, axis=-1)
    o = jnp.einsum('bhsm,bmhd->bshd', p.astype(vm.dtype), vm)
    return o.reshape(b, seq, MEM_WIDTH)


def _swiglu(x, w_in, w_out):
    gu = x @ w_in
    return (jax.nn.silu(gu[..., :D_FF]) * gu[..., D_FF:]) @ w_out


def setup_inputs(seed: int = 0) -> dict:
    key = jax.random.key(seed)
    ks = jax.random.split(key, 17)
    f32 = jnp.float32
    n = lambda k, shape, s: jax.random.normal(k, shape, f32) * s
    return {
        "x": n(ks[0], (BATCH, SEQ, D_MODEL), 1.0),
        "mem": n(ks[1], (BATCH, MEM_LEN, D_MODEL), 1.0),
        "g_mix": 1.0 + n(ks[2], (DEPTH, D_MODEL), 0.02),
        "w_in": n(ks[3], (DEPTH, D_MODEL, IN_WIDTH), D_MODEL ** -0.5),
        "w_mem_kv": n(ks[4], (DEPTH, D_MODEL, 2 * MEM_WIDTH), D_MODEL ** -0.5),
        "w_o": n(ks[5], (DEPTH, OUT_IN_WIDTH, D_MODEL), OUT_IN_WIDTH ** -0.5),
        "g_ffn": 1.0 + n(ks[6], (DEPTH, D_MODEL), 0.02),
        "w_ffn_in": n(ks[7], (DEPTH, D_MODEL, 2 * D_FF), D_MODEL ** -0.5),
        "w_ffn_out": n(ks[8], (DEPTH, D_FF, D_MODEL), D_FF ** -0.5),
        "lam_q1": n(ks[9], (N_DIFF, DIFF_HALF_DIM), LAMBDA_STD),
        "lam_k1": n(ks[10], (N_DIFF, DIFF_HALF_DIM), LAMBDA_STD),
        "lam_q2": n(ks[11], (N_DIFF, DIFF_HALF_DIM), LAMBDA_STD),
        "lam_k2": n(ks[12], (N_DIFF, DIFF_HALF_DIM), LAMBDA_STD),
        "g_diff_head": 1.0 + n(ks[13], (N_DIFF, 2 * DIFF_HALF_DIM), 0.02),
        "conv_w": n(ks[14], (N_CONV, CONV_WIDTH, D_MODEL), CONV_WIDTH ** -0.5),
        "g_mem": 1.0 + n(ks[15], (D_MODEL,), 0.02),
        "g_final": 1.0 + n(ks[16], (D_MODEL,), 0.02),
    }


def reference(x, mem, g_mix, w_in, w_mem_kv, w_o, g_ffn, w_ffn_in, w_ffn_out,
              lam_q1, lam_k1, lam_q2, lam_k2, g_diff_head, conv_w, g_mem, g_final):
    mem_n = _rmsnorm(mem, g_mem)
    for i in range(DEPTH):
        kind = i % N_MIXERS
        j = i // N_MIXERS
        hn = _rmsnorm(x, g_mix[i])
        proj = hn @ w_in[i]
        cols, mq = proj[..., :3 * MIX_WIDTH], proj[..., 3 * MIX_WIDTH:]
        if kind == 0:
            mix = _stick_breaking(cols)
        elif kind == 1:
            mix = _diff_attention(cols, lam_q1[j], lam_k1[j], lam_q2[j], lam_k2[j],
                                  g_diff_head[j], i)
        else:
            mix = _short_conv(cols, conv_w[j])
        mo = _memory_attention(mq, mem_n, w_mem_kv[i])
        x = x + jnp.concatenate([mix, mo], axis=-1) @ w_o[i]
        x = x + _swiglu(_rmsnorm(x, g_ffn[i]), w_ffn_in[i], w_ffn_out[i])
    return _rmsnorm(x, g_final)
```

```python
import contextlib
import math
import numpy as np
import ml_dtypes
import concourse.bass as bass
import concourse.mybir as mybir
from concourse.bass_utils import run_bass_kernel_spmd

F32 = mybir.dt.float32
BF16 = mybir.dt.bfloat16
AF = mybir.ActivationFunctionType
ALU = mybir.AluOpType
NPBF = ml_dtypes.bfloat16

D = 1024
SEQ = 16384
BATCH = 2
DEPTH = 4
INW = 3328
DFF = 2816
NJ = DFF // 128
NCORES = 8
EPS = 1e-6
HEPS = 1e-5

ENGS = ['pe', 'act', 'dve', 'pool', 'sp']


class Buf:
    __slots__ = ('name', 'w', 'r', 'semkey', 'semcnt')

    def __init__(self, name):
        self.name = name
        self.w = None
        self.r = {}
        self.semkey = None
        self.semcnt = 0


class Prog:
    SAME_ENG_WAIT = True

    def __init__(self, nc):
        self.nc = nc
        self.pstack = contextlib.ExitStack()
        self.semh = {}
        for k in ENGS[:4] + ['cc']:
            self.semh[k] = self.pstack.enter_context(nc.semaphore(k))
        self.cnt = {e: 0 for e in ENGS}
        self.cccnt = 0
        self.waited = {e: {} for e in ENGS}
        self.sem_pool = []
        self.ndma = 0
        self.ninst = 0
        self.stage_id = 0
        self._rank = {}
        self.begin_stage()

    def begin_stage(self):
        self.sstack = contextlib.ExitStack()
        self.ops = {e: [] for e in ENGS}
        self.stage_bufs = []
        self.stage_id += 1

    def sb(self, name, shape, dt):
        return self.sstack.enter_context(self.nc.sbuf_tensor('s%d_%s' % (self.stage_id, name), list(shape), dt))

    def ps(self, name, shape, dt):
        return self.sstack.enter_context(self.nc.psum_tensor('p%d_%s' % (self.stage_id, name), list(shape), dt))

    def rank(self, e, ename):
        if ename not in self._rank:
            self._rank[ename] = e.partition_id() % 4
        return self._rank[ename]

    def _deps(self, eng, reads, writes):
        need = {}
        for b in reads:
            if b.w is not None:
                k, v = b.w
                if need.get(k, 0) < v:
                    need[k] = v
        for b in writes:
            if b.w is not None:
                k, v = b.w
                if need.get(k, 0) < v:
                    need[k] = v
            for k, v in b.r.items():
                if need.get(k, 0) < v:
                    need[k] = v
        out = []
        wd = self.waited[eng]
        for k, v in need.items():
            if k == eng and (eng == 'pe' or not self.SAME_ENG_WAIT):
                continue
            if wd.get(k, 0) >= v:
                continue
            wd[k] = v
            out.append((k, v))
        return out

    def _mark(self, tok, reads, writes):
        k, v = tok
        for b in reads:
            if b.r.get(k, 0) < v:
                b.r[k] = v
        for b in writes:
            b.w = tok
            b.r = {}

    def op(self, eng, fns, reads=(), writes=()):
        if callable(fns):
            fns = [fns]
        waits = self._deps(eng, reads, writes)
        self.cnt[eng] += 1
        tok = (eng, self.cnt[eng])
        self.ops[eng].append((waits, fns, (eng, 1)))
        self._mark(tok, reads, writes)
        self.ninst += len(fns)
        return tok

    def _dma_sem(self, sembuf):
        if sembuf.semkey is None:
            if self.sem_pool:
                sembuf.semkey, sembuf.semcnt = self.sem_pool.pop()
            else:
                sembuf.semkey = 'dma%d' % self.ndma
                self.ndma += 1
                self.semh[sembuf.semkey] = self.pstack.enter_context(self.nc.semaphore(sembuf.semkey))
                sembuf.semcnt = 0
            self.stage_bufs.append(sembuf)

    def dma(self, queue, out_ap, in_ap, sembuf, reads=(), writes=(), is_output=False):
        waits = self._deps(queue, reads, writes)
        self._dma_sem(sembuf)
        sembuf.semcnt += 16
        tok = (sembuf.semkey, sembuf.semcnt)

        def fn(e, o=out_ap, i=in_ap, q=queue):
            oo = o(self.rank(e, q)) if callable(o) else o
            ii = i(self.rank(e, q)) if callable(i) else i
            return e.dma_start(out=oo, in_=ii)
        self.ops[queue].append((waits, [fn], (sembuf.semkey, 16)))
        self._mark(tok, reads, writes)
        self.ninst += 1
        return tok

    def coll(self, in_ap, out_ap, reads, writes, groups):
        waits = self._deps('pool', reads, writes)
        self.cccnt += 1
        tok = ('cc', self.cccnt)
        self.ops['pool'].append((waits, [lambda e: e.collective_compute(
            "AllGather", ALU.bypass, replica_groups=groups, ins=[in_ap], outs=[out_ap])], ('cc', 1)))
        self._mark(tok, reads, writes)
        return tok

    def end_stage(self):
        targets = {e: self.cnt[e] for e in ENGS[:4]}
        for b in self.stage_bufs:
            targets[b.semkey] = b.semcnt
        for e in ENGS:
            waits = []
            for k, v in targets.items():
                if k == e or v == 0:
                    continue
                if self.waited[e].get(k, 0) >= v:
                    continue
                self.waited[e][k] = v
                waits.append((k, v))
            self.ops[e].append((waits, [], None))
        semh = self.semh
        ops = self.ops

        def run(ename, e):
            for waits, fns, inc in ops[ename]:
                for k, v in waits:
                    e.wait_ge(semh[k], v)
                ins = None
                for fn in fns:
                    ins = fn(e)
                if inc is not None:
                    ins.then_inc(semh[inc[0]], inc[1])

        with self.nc.Block() as block:
            @block.sync
            def _(e):
                run('sp', e)

            @block.tensor
            def _(e):
                run('pe', e)

            @block.scalar
            def _(e):
                run('act', e)

            @block.vector
            def _(e):
                run('dve', e)

            @block.gpsimd
            def _(e):
                run('pool', e)
        for b in self.stage_bufs:
            self.sem_pool.append((b.semkey, b.semcnt))
            b.semkey = None
        self.sstack.close()
        self.begin_stage()

    def finish(self):
        self.sstack.close()
        self.pstack.close()


def _mm(out, lhsT, rhs, start, stop, skip=False):
    return lambda e: e.matmul(out, lhsT, rhs, start=start, stop=stop, skip_group_check=skip)


def _tp(out, in_, ident):
    return lambda e: e.transpose(out, in_, ident)


class Ring:
    def __init__(self, p, name, n, shape, dt, psum=False):
        self.t = [(p.ps if psum else p.sb)('%s%d' % (name, i), shape, dt) for i in range(n)]
        self.b = [Buf('%s%d' % (name, i)) for i in range(n)]
        self.i = 0
        self.n = n

    def next(self):
        i = self.i
        self.i = (i + 1) % self.n
        return self.t[i], self.b[i]


def _copy(p, eng, out, in_, reads, writes, scale=None):
    if eng == 'act':
        if scale is None:
            return p.op('act', lambda e: e.activation(out=out, in_=in_, func=AF.Copy), reads, writes)
        return p.op('act', lambda e: e.activation(out=out, in_=in_, func=AF.Copy, scale=float(scale)),
                    reads, writes)
    if scale is None:
        return p.op(eng, lambda e: e.tensor_copy(out, in_), reads, writes)
    return p.op(eng, lambda e: e.tensor_scalar_mul(out, in_, float(scale)), reads, writes)


class Norm:
    def __init__(self, p, ident, bident, tag):
        self.p = p
        self.ident, self.bident = ident, bident
        self.junk = Ring(p, tag + "junk", 2, [128, D], BF16)
        self.ss = Ring(p, tag + "ss", 2, [128, 1], F32)
        self.sd = Ring(p, tag + "sd", 2, [128, 1], F32)
        self.rs = Ring(p, tag + "rs", 2, [128, 1], F32)
        self.hn = Ring(p, tag + "hn", 2, [128, D], BF16)

    def rstd(self, x_ap, bx, eps_t, beps=None):
        p = self.p
        jk, bjk = self.junk.next()
        ss, bss = self.ss.next()
        sd, bsd = self.sd.next()
        rs, brs = self.rs.next()
        p.op('act', lambda e: e.activation(out=jk[:], in_=x_ap, func=AF.Square, accum_out=ss[:]),
             [bx], [bjk, bss])
        p.op('act', lambda e: e.activation(out=sd[:], in_=ss[:], func=AF.Sqrt, scale=1.0 / D,
                                           bias=eps_t[:]), [bss] + ([beps] if beps else []), [bsd])
        p.op('dve', lambda e: e.reciprocal(rs[:], sd[:]), [bsd], [brs])
        return rs, brs

    def run(self, x_ap, bx, g, bg, eps_t, tp_ps, btp, hnT_ap, bhnT, copy_eng, beps=None):
        p = self.p
        rs, brs = self.rstd(x_ap, bx, eps_t, beps)
        hn, bhn = self.hn.next()
        p.op('dve', lambda e: e.scalar_tensor_tensor(out=hn[:], in0=x_ap, scalar=rs[:, 0:1], in1=g[:],
                                                     op0=ALU.mult, op1=ALU.mult),
             [bx, brs, bg], [bhn])
        ident = self.ident
        p.op('pe', [_tp(tp_ps[:, kc, :], hn[:, kc * 128:(kc + 1) * 128], ident[:]) for kc in range(8)],
             [bhn, self.bident], [btp])
        _copy(p, copy_eng, hnT_ap, tp_ps[:, :, :], [btp], [bhnT])


WL = [8 * INW, 8 * 512, 10 * D, NJ * 2048, NJ * D]
WOFF = [0]
for _w in WL:
    WOFF.append(WOFF[-1] + _w)
LW = WOFF[-1]
TC = SEQ // 4
NST = TC // 512
GROUPS = [[0, 1, 2, 3], [4, 5, 6, 7]]
NEG = -200.0
GV_MIX, GV_FFN, GV_MEM, GV_FIN = 0, 4, 8, 9


class Dram:
    pass


def st_cast(p, dr):
    F = DEPTH * LW
    chunk = F // 128
    rin = Ring(p, "cin", 4, [128, chunk], F32)
    rout = Ring(p, "cout", 4, [128, chunk], BF16)
    engs = ['dve', 'act']
    for i in range(128):
        ti, bi = rin.next()
        to, bo = rout.next()
        p.dma('sp', ti[:], dr.wslab[:, i * chunk:(i + 1) * chunk], bi, writes=[bi])
        _copy(p, engs[i % 2], to[:], ti[:], [bi], [bo])
        p.dma('pool', dr.wbf[:, i * chunk:(i + 1) * chunk], to[:], bo, reads=[bo])
    p.end_stage()


def wsl(dr, l, i):
    o = l * LW
    return dr.wbf[:, o + WOFF[i]:o + WOFF[i + 1]]


def st_pre(p, dr, l, x_src):
    kind = l % 3
    NT, W = 4, 512
    w_in = wsl(dr, l, 0)
    w_kv = wsl(dr, l, 1)
    ident = p.sb("ident", [128, 128], BF16); bident = Buf("ident")
    g_sb = p.sb("g", [128, D], F32); bg = Buf("g")
    gmem_sb = p.sb("gmem", [128, D], F32); bgmem = Buf("gmem")
    eps_t = p.sb("eps", [128, 1], F32); beps = Buf("eps")
    w_sb = p.sb("w_in", [128, 8, INW], BF16)
    bw = [Buf("w_in%d" % kc) for kc in range(8)]
    wkv_sb = p.sb("w_kv", [128, 8, 512], BF16); bwkv = Buf("w_kv")
    p.dma('sp', ident[:], dr.ident, bident, writes=[bident])
    p.dma('sp', g_sb[:], dr.gvec[:, (GV_MIX + l) * D:(GV_MIX + l + 1) * D], bg, writes=[bg])
    p.dma('sp', gmem_sb[:], dr.gvec[:, GV_MEM * D:(GV_MEM + 1) * D], bgmem, writes=[bgmem])
    p.op('pool', lambda e: e.memset(eps_t[:], EPS), [], [beps])
    p.dma('sp', wkv_sb[:].rearrange("p k c -> p (k c)"), w_kv, bwkv, writes=[bwkv])
    for kc in range(8):
        p.dma('sp', w_sb[:, kc, :], w_in[:, kc * INW:(kc + 1) * INW], bw[kc], writes=[bw[kc]])
    if kind == 2:
        cw_sb = p.sb("cw", [128, 8, 3], F32); bcw = Buf("cw")
        p.dma('sp', cw_sb[:].rearrange("p c k -> p (c k)"), dr.cw, bcw, writes=[bcw])

    tp_ps = p.ps("tp", [128, 8, 128], BF16); btp = Buf("tp")
    acc = Ring(p, "acc", 4, [128, 512], F32, psum=True)
    sc_ps = Ring(p, "sc", 2, [128, 512], F32, psum=True)
    mo_ps = p.ps("mo", [128, 512], F32); bmo_ps = Buf("mo_ps")
    nrm = Norm(p, ident, bident, "n")
    memx = Ring(p, "memx", 2, [128, D], F32)
    memT = p.sb("memT", [128, 8, 256], BF16); bmemT = [Buf("memT0"), Buf("memT1")]
    kmT = p.sb("kmT", [128, 2, 256], BF16); bkmT = Buf("kmT")
    vm = p.sb("vm", [128, 2, 4, 65], BF16); bvm = Buf("vm")
    p.op('pool', lambda e: e.memset(vm[:].rearrange("p a h d -> p (a h d)"), 1.0), [], [bvm])
    for mc in range(2):
        t, b = memx.next()
        p.dma('sp', t[:], dr.mem[mc * 128:(mc + 1) * 128, :], b, writes=[b])
        nrm.run(t[:], b, gmem_sb, bgmem, eps_t, tp_ps, btp, memT[:, :, mc * 128:(mc + 1) * 128],
                bmemT[mc], 'act', beps)
    for cc in range(2):
        ps, bps = acc.next()
        p.op('pe', [_mm(ps[:, 0:256], wkv_sb[:, kc, cc * 128:(cc + 1) * 128], memT[:, kc, :],
                        kc == 0, kc == 7) for kc in range(8)], [bwkv] + bmemT, [bps])
        _copy(p, 'dve', kmT[:, cc, :], ps[:, 0:256], [bps], [bkmT])
    for mc in range(2):
        ps, bps = acc.next()
        p.op('pe', [_mm(ps[:, 0:256], memT[:, kc, mc * 128:(mc + 1) * 128], wkv_sb[:, kc, 256:512],
                        kc == 0, kc == 7) for kc in range(8)], [bwkv] + bmemT, [bps])
        _copy(p, 'act', vm[:, mc, :, 0:64], ps[:, 0:256].rearrange("p (h d) -> p h d", h=4), [bps], [bvm])

    xt = Ring(p, "xt", 1 if kind == 2 else 2, [128, NT, D], F32)
    hnT = Ring(p, "hnT", 2, [128, 8, W], BF16)
    stg = Ring(p, "stg", 8, [128, W], BF16)
    mqT = Ring(p, "mqT", 2, [128, 2, W], BF16)
    pT = Ring(p, "pT", 1 if kind == 2 else 2, [128, 4, 2, W], BF16)
    rsm = Ring(p, "rsm", 2, [128, 4], F32)
    mo_bf = Ring(p, "mobf", 2, [128, 256], BF16)
    moT_st = Ring(p, "moTst", 2, [128, 2, W], BF16)
    if kind == 2:
        gb = Ring(p, "gb", 2, [128, W], F32)
        gc = Ring(p, "gc", 2, [128, W], F32)
        uu = [p.sb("uu%d" % c, [128, W + 2], F32) for c in range(8)]
        buu = [Buf("uu%d" % c) for c in range(8)]
        yy = Ring(p, "yy", 2, [128, W], F32)
        hx = p.sb("hx", [128, 1, D], F32); bhx = Buf("hx")
        hnTh = p.sb("hnTh", [128, 8, 128], BF16); bhnTh = Buf("hnTh")
        hcr = Ring(p, "hc", 2, [128, D], F32)
        hsel = p.sb("hsel", [128, 4], F32); bhsel = Buf("hsel")
        p.dma('sp', hsel[:], dr.hsel, bhsel, writes=[bhsel])
    evac = ['act', 'dve']

    def proj_fm(chunk, hT, bhT, width):
        ps, bps = acc.next()
        p.op('pe', [_mm(ps[:, 0:width], w_sb[:, kc, chunk * 128:(chunk + 1) * 128], hT[:, kc, 0:width],
                        kc == 0, kc == 7) for kc in range(8)], bw + [bhT], [bps])
        return ps, bps

    for st in range(NST):
        t0 = st * W
        if kind == 2:
            pc, hf = st // 2, st % 2
            for i in range(4):
                ct, bct = hcr.next()
                if i < 3:
                    r0 = 128 + pc * 1024 + i * 256 + hf * 128
                    rb = [dr.b_hgath[pc]]
                elif st == 0:
                    r0, rb = 0, []
                else:
                    r0 = 128 + ((st - 1) // 2) * 1024 + 3 * 256 + ((st - 1) % 2) * 128
                    rb = [dr.b_hgath[(st - 1) // 2]]
                p.dma('sp', ct[:], dr.hgath[r0:r0 + 128, :], bct, reads=rb, writes=[bct])
                if i == 0:
                    p.op('dve', lambda e, ct=ct, i=i: e.tensor_scalar_mul(hx[:, 0, :], ct[:], hsel[:, i:i + 1]),
                         [bct, bhsel], [bhx])
                else:
                    p.op('dve', lambda e, ct=ct, i=i: e.scalar_tensor_tensor(
                        out=hx[:, 0, :], in0=ct[:], scalar=hsel[:, i:i + 1], in1=hx[:, 0, :],
                        op0=ALU.mult, op1=ALU.add), [bct, bhsel, bhx], [bhx])
            nrm.run(hx[:, 0, :], bhx, g_sb, bg, eps_t, tp_ps, btp, hnTh[:, :, :], bhnTh, 'act', beps)
            for c in range(8):
                psc, bpsc = proj_fm(8 + c, hnTh, bhnTh, 128)
                t, b = gc.next()
                _copy(p, 'act', t[:, 0:128], psc[:, 0:128], [bpsc], [b])
                psh, bpsh = proj_fm(16 + c, hnTh, bhnTh, 128)
                p.op('dve', lambda e, t=t, psh=psh, c=c: e.tensor_tensor(
                    out=uu[c][:, 0:2], in0=psh[:, 126:128], in1=t[:, 126:128], op=ALU.mult),
                    [bpsh, b], [buu[c]])
        xtile, bxt = xt.next()
        p.dma('sp', xtile[:], x_src[t0:t0 + W, :].rearrange("(j p) d -> p j d", p=128), bxt, writes=[bxt])
        hT, bhT = hnT.next()
        for j in range(NT):
            nrm.run(xtile[:, j, :], bxt, g_sb, bg, eps_t, tp_ps, btp, hT[:, :, j * 128:(j + 1) * 128],
                    bhT, evac[j % 2], beps)
        if kind == 2:
            for c in range(8):
                psc, bpsc = proj_fm(8 + c, hT, bhT, W)
                tgc, bgc = gc.next()
                _copy(p, 'act', tgc[:], psc[:, :], [bpsc], [bgc])
                psh, bpsh = proj_fm(16 + c, hT, bhT, W)
                p.op('dve', lambda e, c=c, psh=psh, tgc=tgc: e.tensor_tensor(
                    out=uu[c][:, 2:W + 2], in0=psh[:, :], in1=tgc[:], op=ALU.mult),
                    [bpsh, bgc], [buu[c]])
                psb, bpsb = proj_fm(c, hT, bhT, W)
                tgb, bgb = gb.next()
                _copy(p, 'act', tgb[:], psb[:, :], [bpsb], [bgb])
                ty, by = yy.next()
                p.op('act', lambda e, c=c, ty=ty: e.activation(out=ty[:], in_=uu[c][:, 0:W], func=AF.Copy,
                                                               scale=cw_sb[:, c, 0:1]), [buu[c], bcw], [by])
                p.op('dve', lambda e, c=c, ty=ty: e.scalar_tensor_tensor(
                    out=ty[:], in0=uu[c][:, 1:W + 1], scalar=cw_sb[:, c, 1:2], in1=ty[:],
                    op0=ALU.mult, op1=ALU.add), [buu[c], bcw, by], [by])
                p.op('dve', lambda e, c=c, ty=ty: e.scalar_tensor_tensor(
                    out=ty[:], in0=uu[c][:, 2:W + 2], scalar=cw_sb[:, c, 2:3], in1=ty[:],
                    op0=ALU.mult, op1=ALU.add), [buu[c], bcw, by], [by])
                so, bso = stg.next()
                p.op('pool', lambda e, so=so, ty=ty, tgb=tgb: e.tensor_tensor(
                    out=so[:], in0=ty[:], in1=tgb[:], op=ALU.mult), [by, bgb], [bso])
                p.dma('sp', dr.mixc[c * 128:(c + 1) * 128, t0:t0 + W], so[:], bso, reads=[bso])
        else:
            snd = dr.qkv_send[l]
            gth = dr.qkv_gath[l]
            for part in range(3):
                bsend = [Buf("snd") for _ in range(8)]
                for c8 in range(8):
                    c = part * 8 + c8
                    ps, bps = proj_fm(c, hT, bhT, W)
                    so, bso = stg.next()
                    _copy(p, evac[c % 2], so[:], ps[:, :], [bps], [bso], scale=(128.0 ** -0.5 if part == 0 else None))
                    r0 = st * 1024 + c8 * 128
                    p.dma('sp', snd[part][r0:r0 + 128, :], so[:], bso, reads=[bso], writes=[bsend[c8]])
                p.coll(snd[part][st * 1024:(st + 1) * 1024, :], gth[part][st],
                       reads=bsend, writes=[dr.b_qkv[l][part][st]], groups=GROUPS)
        mq, bmq = mqT.next()
        for cc in range(2):
            ps, bps = proj_fm(24 + cc, hT, bhT, W)
            _copy(p, evac[cc], mq[:, cc, :], ps[:, :], [bps], [bmq])
        pt, bpt = pT.next()
        for h in range(4):
            cc, po = h // 2, (h % 2) * 64
            for mc in range(2):
                ps, bps = sc_ps.next()
                p.op('pe', [_mm(ps[:, :], kmT[po:po + 64, cc, mc * 128:(mc + 1) * 128],
                                mq[po:po + 64, cc, :], True, True)], [bkmT, bmq], [bps])
                p.op('act', lambda e, ps=ps, h=h, mc=mc, pt=pt: e.activation(
                    out=pt[:, h, mc, :], in_=ps[:, :], func=AF.Exp, scale=0.125), [bps], [bpt])
        mst, bmst = moT_st.next()
        for j in range(NT):
            mms = []
            for h in range(4):
                for mc in range(2):
                    mms.append(_mm(mo_ps[:, h * 65:(h + 1) * 65], pt[:, h, mc, j * 128:(j + 1) * 128],
                                   vm[:, mc, h, :], (h == 0 and mc == 0), (h == 3 and mc == 1), skip=True))
            p.op('pe', mms, [bpt, bvm], [bmo_ps])
            rs, brs = rsm.next()
            mo3 = mo_ps[:, 0:260].rearrange("p (h d) -> p h d", h=4)
            p.op('dve', lambda e, rs=rs, mo3=mo3: e.reciprocal(rs[:], mo3[:, :, 64]), [bmo_ps], [brs])
            mb, bmb = mo_bf.next()
            for h in range(4):
                p.op('dve', lambda e, h=h, mb=mb, rs=rs: e.tensor_scalar_mul(
                    mb[:, h * 64:(h + 1) * 64], mo_ps[:, h * 65:h * 65 + 64], rs[:, h:h + 1]),
                    [bmo_ps, brs], [bmb])
            p.op('pe', [_tp(tp_ps[:, cc, :], mb[:, cc * 128:(cc + 1) * 128], ident[:]) for cc in range(2)],
                 [bmb, bident], [btp])
            _copy(p, 'act', mst[:, :, j * 128:(j + 1) * 128], tp_ps[:, 0:2, :], [btp], [bmst])
        for cc in range(2):
            p.dma('sp', dr.moT[cc * 128:(cc + 1) * 128, t0:t0 + W], mst[:, cc, :], bmst, reads=[bmst])
    p.end_stage()


def st_post(p, dr, l, x_src, x_dst):
    final = (l == DEPTH - 1)
    kind = l % 3
    NT, W = 4, 512
    w_o, w_fi, w_fo = wsl(dr, l, 2), wsl(dr, l, 3), wsl(dr, l, 4)
    ident = p.sb("ident", [128, 128], BF16); bident = Buf("ident")
    g_sb = p.sb("g", [128, D], F32); bg = Buf("g")
    eps_t = p.sb("eps", [128, 1], F32); beps = Buf("eps")
    wo_sb = p.sb("w_o", [128, 10, D], BF16); bwo = Buf("w_o")
    wfo_sb = p.sb("w_fo", [128, NJ, D], BF16)
    bwfo = [Buf("w_fo%d" % i) for i in range(2)]
    p.dma('sp', ident[:], dr.ident, bident, writes=[bident])
    p.dma('sp', g_sb[:], dr.gvec[:, (GV_FFN + l) * D:(GV_FFN + l + 1) * D], bg, writes=[bg])
    p.op('pool', lambda e: e.memset(eps_t[:], EPS), [], [beps])
    p.dma('sp', wo_sb[:].rearrange("p k c -> p (k c)"), w_o, bwo, writes=[bwo])
    for i in range(2):
        p.dma('sp', wfo_sb[:, i * 11:(i + 1) * 11, :].rearrange("p k c -> p (k c)"),
              w_fo[:, i * 11 * D:(i + 1) * 11 * D], bwfo[i], writes=[bwfo[i]])
    if final:
        gfin_sb = p.sb("gfin", [128, D], F32); bgfin = Buf("gfin")
        p.dma('sp', gfin_sb[:], dr.gvec[:, GV_FIN * D:(GV_FIN + 1) * D], bgfin, writes=[bgfin])
        ofin = Ring(p, "ofin", 2, [128, D], F32)
    tp_ps = p.ps("tp", [128, 8, 128], BF16); btp = Buf("tp")
    acc = Ring(p, "acc", 3, [128, 512], F32, psum=True)
    gu = Ring(p, "gu", 4, [128, 512], F32, psum=True)
    nrm = Norm(p, ident, bident, "n")
    xt = Ring(p, "xt", 2, [128, NT, D], F32)
    mcT = Ring(p, "mcT", 1, [128, 10, W], BF16)
    hnT = Ring(p, "hnT", 1, [128, 8, W], BF16)
    hT = Ring(p, "hT", 1, [128, NJ, W], BF16)
    wslr = Ring(p, "wsl", 4, [128, 8, 2, 128], BF16)
    sg = Ring(p, "sg", 2, [128, W], F32)
    evac = ['act', 'dve']
    halo_bufs = []
    for st in range(NST):
        t0 = st * W
        xtile, bxt = xt.next()
        p.dma('sp', xtile[:], x_src[t0:t0 + W, :].rearrange("(j p) d -> p j d", p=128), bxt, writes=[bxt])
        mc, bmc = mcT.next()
        if kind == 2:
            p.dma('sp', mc[:, 0:8, :], dr.mixc[:, t0:t0 + W].rearrange("(k p) t -> p k t", p=128), bmc, writes=[bmc])
        else:
            og = dr.o_gath[l]
            p.dma('sp', mc[:, 0:8, :],
                  lambda r, st=st, og=og: og[st].rearrange("(k p) t -> p k t", p=128)[:, :, bass.ds(r * 512, 512)],
                  bmc, reads=[dr.b_o[l][st]], writes=[bmc])
        p.dma('sp', mc[:, 8:10, :], dr.moT[:, t0:t0 + W].rearrange("(k p) t -> p k t", p=128), bmc, writes=[bmc])
        hn, bhn = hnT.next()
        for j in range(NT):
            for g in range(2):
                ps, bps = acc.next()
                p.op('pe', [_mm(ps[:, :], mc[:, kc, j * 128:(j + 1) * 128], wo_sb[:, kc, g * 512:(g + 1) * 512],
                                kc == 0, kc == 9) for kc in range(10)], [bmc, bwo], [bps])
                p.op('dve', lambda e, ps=ps, j=j, g=g, xtile=xtile: e.tensor_tensor(
                    out=xtile[:, j, g * 512:(g + 1) * 512], in0=ps[:, :], in1=xtile[:, j, g * 512:(g + 1) * 512],
                    op=ALU.add), [bps, bxt], [bxt])
            nrm.run(xtile[:, j, :], bxt, g_sb, bg, eps_t, tp_ps, btp, hn[:, :, j * 128:(j + 1) * 128],
                    bhn, evac[j % 2], beps)
        h, bh = hT.next()
        for jj in range(NJ):
            ws, bws = wslr.next()
            p.dma('sp', ws[:].rearrange("p k s c -> p (k s c)"), w_fi[:, jj * 2048:(jj + 1) * 2048], bws, writes=[bws])
            psg, bpsg = gu.next()
            p.op('pe', [_mm(psg[:, :], ws[:, kc, 0, :], hn[:, kc, :], kc == 0, kc == 7) for kc in range(8)],
                 [bws, bhn], [bpsg])
            psu, bpsu = gu.next()
            p.op('pe', [_mm(psu[:, :], ws[:, kc, 1, :], hn[:, kc, :], kc == 0, kc == 7) for kc in range(8)],
                 [bws, bhn], [bpsu])
            s, bs = sg.next()
            p.op('act', lambda e, s=s, psg=psg: e.activation(out=s[:], in_=psg[:, :], func=AF.Silu), [bpsg], [bs])
            p.op('dve', lambda e, s=s, psu=psu, jj=jj, h=h: e.tensor_tensor(
                out=h[:, jj, :], in0=psu[:, :], in1=s[:], op=ALU.mult), [bpsu, bs], [bh])
        for j in range(NT):
            for g in range(2):
                ps, bps = acc.next()
                p.op('pe', [_mm(ps[:, :], h[:, jj, j * 128:(j + 1) * 128], wfo_sb[:, jj, g * 512:(g + 1) * 512],
                                jj == 0, jj == NJ - 1) for jj in range(NJ)], [bh] + bwfo, [bps])
                p.op('dve', lambda e, ps=ps, j=j, g=g, xtile=xtile: e.tensor_tensor(
                    out=xtile[:, j, g * 512:(g + 1) * 512], in0=ps[:, :], in1=xtile[:, j, g * 512:(g + 1) * 512],
                    op=ALU.add), [bps, bxt], [bxt])
            if final:
                rs, brs = nrm.rstd(xtile[:, j, :], bxt, eps_t, beps)
                o, bo = ofin.next()
                p.op('dve', lambda e, o=o, j=j, xtile=xtile, rs=rs: e.scalar_tensor_tensor(
                    out=o[:], in0=xtile[:, j, :], scalar=rs[:, 0:1], in1=gfin_sb[:], op0=ALU.mult, op1=ALU.mult),
                    [bxt, brs, bgfin], [bo])
                p.dma('sp', x_dst[t0 + j * 128:t0 + (j + 1) * 128, :], o[:], bo, reads=[bo])
        if not final:
            p.dma('sp', x_dst[t0:t0 + W, :].rearrange("(j p) d -> p j d", p=128), xtile[:], bxt, reads=[bxt])
        if l % 3 == 1 and l + 1 < DEPTH:
            bh_ = Buf("hs")
            p.dma('sp', dr.hsend[st * 128:(st + 1) * 128, :], xtile[:, 3, :], bxt, reads=[bxt], writes=[bh_])
            halo_bufs.append(bh_)
            if st % 2 == 1:
                pc = st // 2
                p.coll(dr.hsend[pc * 256:(pc + 1) * 256, :], dr.hgath[128 + pc * 1024:128 + (pc + 1) * 1024, :],
                       reads=halo_bufs[-2:], writes=[dr.b_hgath[pc]], groups=GROUPS)
    p.end_stage()


def _load_kv_piece(p, dr, l, part, st, off_fn, out_ap3, sembuf, writes):
    gth = dr.qkv_gath[l][part][st]

    def src(r, gth=gth):
        v3 = gth.rearrange("(r f) t -> f r t", r=4)
        return v3[bass.ds(off_fn(r), 128), :, :]
    p.dma('sp', out_ap3, src, sembuf, reads=[dr.b_qkv[l][part][st]], writes=writes)


def st_sb(p, dr, l):
    S, NH = SEQ, 2
    tri = p.sb("tri", [128, 128], BF16); btri = Buf("tri")
    ones = p.sb("ones", [128, 128], BF16); bones = Buf("ones")
    ident = p.sb("ident", [128, 128], BF16); bident = Buf("ident")
    msk = p.sb("msk", [128, 896], F32); bmsk = Buf("msk")
    one1 = p.sb("one1", [128, 1], F32); bone1 = Buf("one1")
    p.dma('sp', tri[:], dr.tri, btri, writes=[btri])
    p.dma('sp', ones[:], dr.ones, bones, writes=[bones])
    p.dma('sp', ident[:], dr.ident, bident, writes=[bident])
    p.dma('sp', msk[:], dr.msk, bmsk, writes=[bmsk])
    p.op('pool', lambda e: e.memset(one1[:], 1.0), [], [bone1])
    NG = S // 512
    kring = Ring(p, "kT", 2, [128, S], BF16)
    vring = Ring(p, "v", 2, [128, S], BF16)
    vtr = Ring(p, "vTp", 2, [128, 2048], BF16)
    qring = Ring(p, "qg", 2, [128, 2048], BF16)
    zps = Ring(p, "z", 2, [128, 512], F32, psum=True)
    accr = Ring(p, "acc", 2, [128, 512], F32, psum=True)
    ops_ = Ring(p, "o", 3, [128, 512], F32, psum=True)
    tp_ps = p.ps("tp", [128, 8, 128], BF16); btp = Buf("tp")
    er = Ring(p, "e", 4, [128, 512], F32)
    ur = Ring(p, "u", 4, [128, 512], BF16)
    ar = Ring(p, "a", 3, [128, 512], BF16)
    gr = Ring(p, "g", 3, [128, 512], F32)
    osb = Ring(p, "osb", 2, [128, 512], BF16)
    su = p.sb("su", [128, 128], BF16); bsu = Buf("su")
    p.op('dve', lambda e: e.tensor_tensor(out=su[:], in0=ones[:], in1=tri[:], op=ALU.subtract), [bones, btri], [bsu])
    heads = {}

    kpb = {}
    vpb = {}

    def alloc_head(h):
        kt, _ = kring.next()
        vt, _ = vring.next()
        hi = kring.i
        if hi not in kpb:
            kpb[hi] = [Buf("kp") for _ in range(NST)]
            vpb[hi] = [Buf("vp") for _ in range(NST)]
        heads[h] = (kt, kpb[hi], vt, vpb[hi])

    def load_kv(h, st):
        kt, bkt, vt, bvt = heads[h]
        _load_kv_piece(p, dr, l, 1, st, lambda r, h=h: r * 256 + 128 * h,
                       kt[:, st * 2048:(st + 1) * 2048].rearrange("p (r t) -> p r t", r=4), bkt[st], [bkt[st]])
        vp, bvp = vtr.next()
        _load_kv_piece(p, dr, l, 2, st, lambda r, h=h: r * 256 + 128 * h,
                       vp[:, :].rearrange("p (r t) -> p r t", r=4), bvp, [bvp])
        for half in range(2):
            p.op('pe', [_tp(tp_ps[:, i, :], vp[:, (half * 8 + i) * 128:(half * 8 + i + 1) * 128], ident[:])
                        for i in range(8)], [bvp, bident], [btp])
            c0 = st * 16 + half * 8
            _copy(p, 'dve', vt[:, c0 * 128:(c0 + 8) * 128].rearrange("p (c d) -> p c d", c=8),
                  tp_ps[:, :, :], [btp], [bvt[st]])

    def load_head(h):
        alloc_head(h)
        for st in range(NST):
            load_kv(h, st)

    pieces = {}
    porder = [(h, st) for h in range(NH) for st in range(NST)]

    def load_qpiece(h, st):
        qg, bqg = qring.next()
        _load_kv_piece(p, dr, l, 0, st, lambda r, h=h: r * 256 + 128 * h,
                       qg[:, :].rearrange("p (r t) -> p r t", r=4), bqg, [bqg])
        pieces[(h, st)] = (qg, bqg)

    steps = []
    for h in range(NH):
        for G in range(NG):
            cs = list(range(4 * G + 3, -1, -1))
            for i, c in enumerate(cs):
                steps.append(dict(h=h, G=G, c=c, first=(i == 0), last=(c == 0), pd=c - 4 * G))
    order = [(h, G) for h in range(NH) for G in range(NG)]
    gidx = {hg: i for i, hg in enumerate(order)}
    alloc_head(0)
    load_kv(0, 0)
    load_qpiece(0, 0)
    gstate = {}
    obufs = {}

    def s12(s):
        h, G, c = s['h'], s['G'], s['c']
        if s['first']:
            if G % 4 == 0:
                if h == 0 and G // 4 + 1 < NST:
                    load_kv(0, G // 4 + 1)
                pi = porder.index((h, G // 4))
                if pi + 1 < len(porder):
                    load_qpiece(*porder[pi + 1])
            if G == NG - 8 and h + 1 < NH:
                load_head(h + 1)
            acc, bacc = accr.next()
            ot, bot = ops_.next()
            gstate[(h, G)] = dict(acc=acc, bacc=bacc, ot=ot, bot=bot)
        kt, bkt, vt, bvt = heads[h]
        qg, bqg = pieces[(h, G // 4)]
        qc0 = (G % 4) * 512
        qlo = 128 * s['pd'] if s['pd'] > 0 else 0
        s['qlo'] = qlo
        z, bz = zps.next()
        p.op('pe', [_mm(z[:, qlo:512], kt[:, c * 128:(c + 1) * 128], qg[:, qc0 + qlo:qc0 + 512], True, True)],
             [bkt[c // 16], bqg], [bz])
        e_, be = er.next()
        p.op('act', lambda e, e_=e_, z=z, qlo=qlo: e.activation(out=e_[:, qlo:512], in_=z[:, qlo:512], func=AF.Exp),
             [bz], [be])
        if s['pd'] >= 0:
            off = 384 - 128 * s['pd']
            p.op('dve', lambda e, e_=e_, qlo=qlo, off=off: e.tensor_tensor(
                out=e_[:, qlo:512], in0=e_[:, qlo:512], in1=msk[:, off + qlo:off + 512], op=ALU.mult),
                [be, bmsk], [be])
        s.update(e=e_, be=be)

    def sLN(s):
        qlo = s['qlo']
        e_, be = s['e'], s['be']
        u_, bu = ur.next()
        p.op('act', lambda e, u_=u_, e_=e_, qlo=qlo: e.activation(out=u_[:, qlo:512], in_=e_[:, qlo:512], func=AF.Ln,
                                                                 bias=1.0, scale=1.0), [be], [bu])
        s.update(u=u_, bu=bu)

    def s34(s):
        h, G, c = s['h'], s['G'], s['c']
        gs = gstate[(h, G)]
        qlo = s['qlo']
        acc, bacc = gs['acc'], gs['bacc']
        u_, bu = s['u'], s['bu']
        p.op('pe', [_mm(acc[:, qlo:512], tri[:, :], u_[:, qlo:512], s['first'], s['last'], skip=True)],
             [btri, bu], [bacc])
        g_, bg_ = gr.next()
        p.op('act', lambda e, g_=g_, acc=acc, qlo=qlo: e.activation(out=g_[:, qlo:512], in_=acc[:, qlo:512],
                                                                    func=AF.Exp, scale=-1.0), [bacc], [bg_])
        a_, ba = ar.next()
        e_, be = s['e'], s['be']
        p.op('dve', lambda e, a_=a_, e_=e_, g_=g_, qlo=qlo: e.tensor_tensor(
            out=a_[:, qlo:512], in0=g_[:, qlo:512], in1=e_[:, qlo:512], op=ALU.mult), [be, bg_], [ba])
        s.update(a=a_, ba=ba)

    def sSU(s):
        if s['last']:
            return
        gs = gstate[(s['h'], s['G'])]
        qlo = s['qlo']
        acc, bacc = gs['acc'], gs['bacc']
        p.op('pe', [_mm(acc[:, qlo:512], su[:, :], s['u'][:, qlo:512], False, False, skip=True)],
             [bsu, s['bu']], [bacc])

    def s5(s):
        h, G, c = s['h'], s['G'], s['c']
        gs = gstate[(h, G)]
        kt, bkt, vt, bvt = heads[h]
        qlo = s['qlo']
        ot, bot = gs['ot'], gs['bot']
        p.op('pe', [_mm(ot[:, qlo:512], vt[:, c * 128:(c + 1) * 128], s['a'][:, qlo:512], s['first'], s['last'],
                        skip=True)], [bvt[c // 16], s['ba']], [bot])
        if s['last']:
            o, bo = osb.next()
            _copy(p, 'dve', o[:], ot[:, :], [bot], [bo])
            st, rr = G // 4, G % 4
            bsnd = Buf("osnd")
            obufs[(h, G)] = bsnd
            p.dma('sp', dr.o_send[l][st * 256 + h * 128:st * 256 + (h + 1) * 128, rr * 512:(rr + 1) * 512], o[:], bo,
                  reads=[bo], writes=[bsnd])
            if h == NH - 1 and rr == 3:
                p.coll(dr.o_send[l][st * 256:(st + 1) * 256, :], dr.o_gath[l][st],
                       reads=[obufs[(hh, 4 * st + r_)] for hh in range(NH) for r_ in range(4)],
                       writes=[dr.b_o[l][st]], groups=GROUPS)

    n = len(steps)
    for i in range(-1, n + 2):
        if 0 <= i + 1 < n:
            s12(steps[i + 1])
        if 0 <= i < n:
            sLN(steps[i])
        if 0 <= i - 2 < n:
            sSU(steps[i - 2])
        if 0 <= i - 1 < n:
            s34(steps[i - 1])
        if 0 <= i - 2 < n:
            s5(steps[i - 2])
    p.end_stage()


def st_diff(p, dr, l):
    S = SEQ
    NC, NG = S // 128, S // 512
    lambda_init = 0.8 - 0.6 * math.exp(-0.3 * l)
    tab = p.sb("tab", [128, 1024], F32); btab = Buf("tab")
    biasT = p.sb("biasT", [128, NC + 1], F32); bbias = Buf("biasT")
    ones = p.sb("ones", [128, 128], BF16); bones = Buf("ones")
    ident = p.sb("ident", [128, 128], BF16); bident = Buf("ident")
    lamv = p.sb("lamv", [128, 4, 128], F32); blamv = Buf("lamv")
    gh = p.sb("gh", [128, 256], F32); bgh = Buf("gh")
    p.dma('sp', tab[:], dr.tab, btab, writes=[btab])
    p.dma('sp', biasT[:], dr.biasT, bbias, writes=[bbias])
    p.dma('sp', ones[:], dr.ones, bones, writes=[bones])
    p.dma('sp', ident[:], dr.ident, bident, writes=[bident])
    p.dma('sp', lamv[:].rearrange("p a d -> p (a d)"), dr.lamv, blamv, writes=[blamv])
    p.dma('sp', gh[:], dr.gh, bgh, writes=[bgh])
    k1 = p.sb("k1T", [128, S], BF16); bk1 = [Buf("k1") for _ in range(NST)]
    k2 = p.sb("k2T", [128, S], BF16); bk2 = [Buf("k2") for _ in range(NST)]
    vs = p.sb("v", [128, NC, 256], BF16); bv = [Buf("v") for _ in range(NST)]
    vtr = Ring(p, "vTp", 2, [128, 2048], BF16)
    zps = Ring(p, "z", 2, [128, 512], F32, psum=True)
    tp_ps = p.ps("tp", [128, 8, 128], BF16); btp = Buf("tp")
    obank = [p.ps("ob%d" % j, [128, 512], F32) for j in range(4)]
    bob = [Buf("ob%d" % j) for j in range(4)]
    sbank = p.ps("sums", [128, 512], F32); bsb = Buf("sums")
    def load_kv(st):
        _load_kv_piece(p, dr, l, 1, st, lambda r: r * 256,
                       k1[:, st * 2048:(st + 1) * 2048].rearrange("p (r t) -> p r t", r=4), bk1[st], [bk1[st]])
        _load_kv_piece(p, dr, l, 1, st, lambda r: r * 256 + 128,
                       k2[:, st * 2048:(st + 1) * 2048].rearrange("p (r t) -> p r t", r=4), bk2[st], [bk2[st]])
        for i in range(2):
            vp, bvp = vtr.next()
            _load_kv_piece(p, dr, l, 2, st, lambda r, i=i: r * 256 + 128 * i,
                           vp[:, :].rearrange("p (r t) -> p r t", r=4), bvp, [bvp])
            for half in range(2):
                p.op('pe', [_tp(tp_ps[:, j, :], vp[:, (half * 8 + j) * 128:(half * 8 + j + 1) * 128], ident[:])
                            for j in range(8)], [bvp, bident], [btp])
                c0 = st * 16 + half * 8
                _copy(p, 'dve' if half else 'act', vs[:, c0:c0 + 8, i * 128:(i + 1) * 128], tp_ps[:, :, :], [btp],
                      [bv[st]])

    prod = p.sb("prod", [128, 128], F32); bprod = Buf("prod")
    sl = p.sb("sl", [128, 2], F32); bsl = Buf("sl")
    el = p.sb("el", [128, 2], F32); bel = Buf("el")
    lam = p.sb("lam", [128, 1], F32); blam = Buf("lam")
    eps_t = p.sb("eps", [128, 1], F32); beps = Buf("eps")
    p.op('pool', lambda e: e.memset(eps_t[:], HEPS), [], [beps])
    for i in range(2):
        p.op('dve', lambda e, i=i: e.tensor_tensor(out=prod[:], in0=lamv[:, 2 * i, :], in1=lamv[:, 2 * i + 1, :],
                                                   op=ALU.mult), [blamv], [bprod])
        p.op('dve', lambda e, i=i: e.reduce_sum(out=sl[:, i:i + 1], in_=prod[:], axis=mybir.AxisListType.X),
             [bprod], [bsl])
    p.op('act', lambda e: e.activation(out=el[:], in_=sl[:], func=AF.Exp), [bsl], [bel])
    p.op('dve', lambda e: e.tensor_tensor(out=lam[:], in0=el[:, 0:1], in1=el[:, 1:2], op=ALU.subtract), [bel], [blam])
    p.op('dve', lambda e: e.tensor_scalar_add(lam[:], lam[:], float(lambda_init)), [blam], [blam])
    p.op('dve', lambda e: e.tensor_scalar_mul(gh[:], gh[:], float(1.0 - lambda_init)), [bgh], [bgh])

    q1r = Ring(p, "q1g", 2, [128, 2048], BF16)
    q2r = Ring(p, "q2g", 2, [128, 2048], BF16)
    tr = Ring(p, "t", 4, [128, 512], F32)
    Er = Ring(p, "E", 6, [128, 512], BF16)
    rsr = Ring(p, "rs", 2, [128, 2], F32)
    tmpr = Ring(p, "tmp", 2, [128, 256], F32)
    ofr = Ring(p, "of", 2, [128, 256], F32)
    jkr = Ring(p, "jk", 2, [128, 256], BF16)
    ssr = Ring(p, "ss", 2, [128, 1], F32)
    sdr = Ring(p, "sd", 2, [128, 1], F32)
    rdr = Ring(p, "rd", 2, [128, 1], F32)
    obr = Ring(p, "obf", 2, [128, 256], BF16)
    ostg = Ring(p, "ostg", 2, [128, 2, 512], BF16)
    pieces = {}

    def load_qpiece(st):
        a, ba = q1r.next()
        b, bb = q2r.next()
        _load_kv_piece(p, dr, l, 0, st, lambda r: r * 256, a[:, :].rearrange("p (r t) -> p r t", r=4), ba, [ba])
        _load_kv_piece(p, dr, l, 0, st, lambda r: r * 256 + 128, b[:, :].rearrange("p (r t) -> p r t", r=4), bb, [bb])
        pieces[st] = (a, ba, b, bb)

    steps = []
    for G in range(NG):
        cs = list(range(4 * G + 3, -1, -1))
        for i, c in enumerate(cs):
            steps.append(dict(G=G, c=c, first=(i == 0), last=(c == 0), pd=c - 4 * G))
    load_kv(0)
    load_qpiece(0)
    obufs = {}

    def s1(s):
        G, c, pd = s['G'], s['c'], s['pd']
        if s['first'] and G % 4 == 0 and G // 4 + 1 < NST:
            load_kv(G // 4 + 1)
            load_qpiece(G // 4 + 1)
        qa, bqa, qb, bqb = pieces[G // 4]
        qc0 = (G % 4) * 512
        qlo = 128 * pd if pd > 0 else 0
        s['qlo'] = qlo
        if pd >= 0:
            off, bi = 384 - 128 * pd, 0
        else:
            off, bi = 512, (-pd) - 1
        Es = []
        for (kk, bkk, qq, bqq) in ((k1, bk1, qa, bqa), (k2, bk2, qb, bqb)):
            z, bz = zps.next()
            p.op('pe', [_mm(z[:, qlo:512], kk[:, c * 128:(c + 1) * 128], qq[:, qc0 + qlo:qc0 + 512], True, True)],
                 [bkk[c // 16], bqq], [bz])
            t, bt = tr.next()
            p.op('dve', lambda e, t=t, z=z, qlo=qlo, off=off: e.tensor_tensor(
                out=t[:, qlo:512], in0=z[:, qlo:512], in1=tab[:, off + qlo:off + 512], op=ALU.add),
                [bz, btab], [bt])
            E, bE = Er.next()
            p.op('act', lambda e, E=E, t=t, qlo=qlo, bi=bi: e.activation(
                out=E[:, qlo:512], in_=t[:, qlo:512], func=AF.Exp, bias=biasT[:, bi:bi + 1], scale=1.0),
                [bt, bbias], [bE])
            Es.append((E, bE))
        s['E'] = Es

    def s2(s):
        G, c, pd = s['G'], s['c'], s['pd']
        (E1, bE1), (E2, bE2) = s['E']
        j0 = pd if pd > 0 else 0
        for jq in range(j0, 4):
            startj = (pd == jq)
            mms = [
                _mm(obank[jq][:, 0:256], E1[:, jq * 128:(jq + 1) * 128], vs[:, c, :], startj, False, skip=True),
                _mm(sbank[:, 2 * jq:2 * jq + 1], E1[:, jq * 128:(jq + 1) * 128], ones[:, 0:1],
                    (pd == 3 and jq == 3), False, skip=True),
                _mm(obank[jq][:, 256:512], E2[:, jq * 128:(jq + 1) * 128], vs[:, c, :], False, s['last'], skip=True),
                _mm(sbank[:, 2 * jq + 1:2 * jq + 2], E2[:, jq * 128:(jq + 1) * 128], ones[:, 0:1],
                    False, s['last'], skip=True),
            ]
            p.op('pe', mms, [bE1, bE2, bv[c // 16], bones], [bob[jq], bsb])
        if s['last']:
            og, bog = ostg.next()
            for jq in range(4):
                rs, brs = rsr.next()
                p.op('dve', lambda e, rs=rs, jq=jq: e.reciprocal(rs[:], sbank[:, 2 * jq:2 * jq + 2]), [bsb], [brs])
                p.op('dve', lambda e, rs=rs: e.tensor_tensor(out=rs[:, 1:2], in0=rs[:, 1:2], in1=lam[:, 0:1],
                                                            op=ALU.mult), [brs, blam], [brs])
                tmp, btmp = tmpr.next()
                p.op('act', lambda e, tmp=tmp, jq=jq, rs=rs: e.activation(
                    out=tmp[:], in_=obank[jq][:, 256:512], func=AF.Copy, scale=rs[:, 1:2]), [bob[jq], brs], [btmp])
                of, bof = ofr.next()
                p.op('dve', lambda e, of=of, jq=jq, rs=rs, tmp=tmp: e.scalar_tensor_tensor(
                    out=of[:], in0=obank[jq][:, 0:256], scalar=rs[:, 0:1], in1=tmp[:], op0=ALU.mult,
                    op1=ALU.subtract), [bob[jq], brs, btmp], [bof])
                jk, bjk = jkr.next()
                ss, bss = ssr.next()
                sd, bsd = sdr.next()
                rd, brd = rdr.next()
                p.op('act', lambda e, jk=jk, of=of, ss=ss: e.activation(out=jk[:], in_=of[:], func=AF.Square,
                                                                        accum_out=ss[:]), [bof], [bjk, bss])
                p.op('act', lambda e, sd=sd, ss=ss: e.activation(out=sd[:], in_=ss[:], func=AF.Sqrt, scale=1.0 / 256,
                                                                 bias=eps_t[:]), [bss, beps], [bsd])
                p.op('dve', lambda e, rd=rd, sd=sd: e.reciprocal(rd[:], sd[:]), [bsd], [brd])
                ob, bobf = obr.next()
                p.op('dve', lambda e, ob=ob, of=of, rd=rd: e.scalar_tensor_tensor(
                    out=ob[:], in0=of[:], scalar=rd[:, 0:1], in1=gh[:], op0=ALU.mult, op1=ALU.mult),
                    [bof, brd, bgh], [bobf])
                p.op('pe', [_tp(tp_ps[:, i, :], ob[:, i * 128:(i + 1) * 128], ident[:]) for i in range(2)],
                     [bobf, bident], [btp])
                _copy(p, 'act', og[:, :, jq * 128:(jq + 1) * 128], tp_ps[:, 0:2, :], [btp], [bog])
            st, rr = G // 4, G % 4
            bsnd = Buf("osnd")
            obufs[G] = bsnd
            p.dma('sp', dr.o_send[l][st * 256:(st + 1) * 256, rr * 512:(rr + 1) * 512].rearrange(
                "(i p) t -> p i t", p=128), og[:], bog, reads=[bog], writes=[bsnd])
            if rr == 3:
                p.coll(dr.o_send[l][st * 256:(st + 1) * 256, :], dr.o_gath[l][st],
                       reads=[obufs[4 * st + r_] for r_ in range(4)], writes=[dr.b_o[l][st]], groups=GROUPS)

    n = len(steps)
    for i in range(n + 1):
        if i < n:
            s1(steps[i])
        if 0 <= i - 1 < n:
            s2(steps[i - 1])
    p.end_stage()


ATT_LAYERS = [l for l in range(DEPTH) if l % 3 != 2]


def build_fused():
    nc = bass.Bass("TRN2", target_bir_lowering=False)
    dt = nc.dram_tensor
    dr = Dram()

    def ext(name, shape, dtype):
        return dt(name, list(shape), dtype, kind="ExternalInput").ap()
    dr.x = ext("x", [TC, D], F32)
    dr.wslab = ext("wslab", [128, DEPTH * LW], F32)
    dr.mem = ext("mem", [256, D], F32)
    dr.gvec = ext("gvec", [128, 10 * D], F32)
    dr.ident = ext("ident", [128, 128], BF16)
    dr.tri = ext("tri", [128, 128], BF16)
    dr.ones = ext("ones", [128, 128], BF16)
    dr.msk = ext("msk", [128, 896], F32)
    dr.tab = ext("tab", [128, 1024], F32)
    dr.biasT = ext("biasT", [128, SEQ // 128 + 1], F32)
    dr.lamv = ext("lamv", [128, 512], F32)
    dr.gh = ext("gh", [128, 256], F32)
    dr.cw = ext("cw", [128, 24], F32)
    dr.hsel = ext("hsel", [128, 4], F32)
    dr.y = dt("y", [TC, D], F32, kind="ExternalOutput").ap()
    dr.wbf = dt("wbf", [128, DEPTH * LW], BF16).ap()
    dr.xa = dt("xa", [TC, D], F32).ap()
    dr.xb = dt("xb", [TC, D], F32).ap()
    dr.moT = dt("moT", [256, TC], BF16).ap()
    dr.mixc = dt("mixc", [D, TC], BF16).ap()
    dr.hsend = dt("hsend", [NST * 128, D], F32).ap()
    dr.hgath = dt("hgath", [128 + 4 * 1024, D], F32).ap()
    dr.b_hgath = [Buf("hg%d" % i) for i in range(4)]
    dr.qkv_send, dr.qkv_gath, dr.o_send, dr.o_gath, dr.b_qkv, dr.b_o = {}, {}, {}, {}, {}, {}
    for l in ATT_LAYERS:
        dr.qkv_send[l] = [dt("snd%d_%d" % (l, i), [NST * 1024, 512], BF16).ap() for i in range(3)]
        dr.qkv_gath[l] = [[dt("gth%d_%d_%d" % (l, i, st), [4096, 512], BF16).ap() for st in range(NST)]
                          for i in range(3)]
        dr.o_send[l] = dt("osnd%d" % l, [NST * 256, 2048], BF16).ap()
        dr.o_gath[l] = [dt("ogth%d_%d" % (l, st), [1024, 2048], BF16).ap() for st in range(NST)]
        dr.b_qkv[l] = [[Buf("g") for _ in range(NST)] for _ in range(3)]
        dr.b_o[l] = [Buf("og") for _ in range(NST)]
    p = Prog(nc)
    zt = p.sb("zt", [128, D], F32); bzt = Buf("zt")
    p.op('pool', lambda e: e.memset(zt[:], 0.0), [], [bzt])
    p.dma('sp', dr.hgath[0:128, :], zt[:], bzt, reads=[bzt])
    st_cast(p, dr)
    xs = [dr.x, dr.xa, dr.xb, dr.xa, dr.y]
    for l in range(DEPTH):
        st_pre(p, dr, l, xs[l])
        if l % 3 == 0:
            st_sb(p, dr, l)
        elif l % 3 == 1:
            st_diff(p, dr, l)
        st_post(p, dr, l, xs[l], xs[l + 1])
    p.finish()
    return nc, p


def _lay(w):
    K = w.shape[0] // 128
    C = w.shape[1]
    return w.reshape(K, 128, C).transpose(1, 0, 2).reshape(128, K * C)


def _lay_fi(w):
    a = w.reshape(8, 128, 2, NJ, 128)
    return a.transpose(1, 3, 0, 2, 4).reshape(128, NJ * 2048)


_NC = []


def kernel(x, mem, g_mix, w_in, w_mem_kv, w_o, g_ffn, w_ffn_in, w_ffn_out,
           lam_q1, lam_k1, lam_q2, lam_k2, g_diff_head, conv_w, g_mem, g_final):
    f32 = np.float32
    x = np.asarray(x, f32); mem = np.asarray(mem, f32)
    g_mix = np.asarray(g_mix, f32); g_ffn = np.asarray(g_ffn, f32)
    g_mem = np.asarray(g_mem, f32); g_final = np.asarray(g_final, f32)
    w_in = np.asarray(w_in, f32); w_mem_kv = np.asarray(w_mem_kv, f32); w_o = np.asarray(w_o, f32)
    w_ffn_in = np.asarray(w_ffn_in, f32); w_ffn_out = np.asarray(w_ffn_out, f32)
    conv_w = np.asarray(conv_w, f32); g_diff_head = np.asarray(g_diff_head, f32)
    lamv = np.concatenate([np.asarray(a, f32)[0] for a in (lam_q1, lam_k1, lam_q2, lam_k2)])
    slab = np.empty((128, DEPTH * LW), f32)
    for l in range(DEPTH):
        o = l * LW
        slab[:, o + WOFF[0]:o + WOFF[1]] = _lay(w_in[l])
        slab[:, o + WOFF[1]:o + WOFF[2]] = _lay(w_mem_kv[l])
        slab[:, o + WOFF[2]:o + WOFF[3]] = _lay(w_o[l])
        slab[:, o + WOFF[3]:o + WOFF[4]] = _lay_fi(w_ffn_in[l])
        slab[:, o + WOFF[4]:o + WOFF[5]] = _lay(w_ffn_out[l])
    gv = np.concatenate([g_mix[0], g_mix[1], g_mix[2], g_mix[3], g_ffn[0], g_ffn[1], g_ffn[2], g_ffn[3],
                         g_mem, g_final])
    gvec = np.ascontiguousarray(np.tile(gv[None, :], (128, 1)))
    k = np.arange(128)[:, None]
    tri = (k >= np.arange(128)[None, :]).astype(f32).astype(NPBF)
    msk = (k < np.arange(896)[None, :] - 384).astype(f32)
    ident = np.eye(128, dtype=f32).astype(NPBF)
    ones = np.ones((128, 128), NPBF)
    cw = np.ascontiguousarray(conv_w[0].T.reshape(8, 128, 3).transpose(1, 0, 2).reshape(128, 24))
    lam_t = np.ascontiguousarray(np.tile(lamv.reshape(1, 512), (128, 1)))
    gh_t = np.ascontiguousarray(np.tile(g_diff_head[0], (128, 1)))
    if not _NC:
        _NC.append(build_fused()[0])
    nc = _NC[0]
    ims = []
    for c in range(NCORES):
        b, r = c // 4, c % 4
        xc = np.ascontiguousarray(x[b].reshape(NST, 4, 512, D)[:, r].reshape(TC, D))
        slope = 2.0 ** (-8.0 * (r + 1) / 4.0)
        dd = (np.arange(1024)[None, :] - 384 - k).astype(np.float64)
        tab = np.where(dd >= 0, -slope * dd, NEG).astype(f32)
        bias = np.tile((-slope * 128.0 * np.arange(SEQ // 128 + 1))[None, :], (128, 1)).astype(f32)
        hsel = np.zeros((128, 4), f32)
        hsel[:, (r + 3) % 4] = 1.0
        ims.append({"hsel": hsel, "x": xc, "wslab": slab, "mem": np.ascontiguousarray(mem[b]), "gvec": gvec, "ident": ident,
                    "tri": tri, "ones": ones, "msk": msk, "tab": tab, "biasT": bias, "lamv": lam_t, "gh": gh_t,
                    "cw": cw})
    res = run_bass_kernel_spmd(nc, ims, core_ids=list(range(NCORES)))
    out = np.empty((BATCH, SEQ, D), f32)
    for c in range(NCORES):
        b, r = c // 4, c % 4
        out[b].reshape(NST, 4, 512, D)[:, r] = np.asarray(res.results[c]["y"]).reshape(NST, 512, D)
    return out
```

```python
import contextlib
import math
import numpy as np
import ml_dtypes
import concourse.bass as bass
import concourse.mybir as mybir
from concourse.bass_utils import run_bass_kernel_spmd

F32 = mybir.dt.float32
BF16 = mybir.dt.bfloat16
AF = mybir.ActivationFunctionType
ALU = mybir.AluOpType
NPBF = ml_dtypes.bfloat16

D = 1024
SEQ = 16384
BATCH = 2
DEPTH = 4
INW = 3328
DFF = 2816
NJ = DFF // 128
NCORES = 8
EPS = 1e-6
HEPS = 1e-5

ENGS = ['pe', 'act', 'dve', 'pool', 'sp']


class Buf:
    __slots__ = ('name', 'w', 'r', 'semkey', 'semcnt')

    def __init__(self, name):
        self.name = name
        self.w = None
        self.r = {}
        self.semkey = None
        self.semcnt = 0


class Prog:
    SAME_ENG_WAIT = True

    def __init__(self, nc):
        self.nc = nc
        self.pstack = contextlib.ExitStack()
        self.semh = {}
        for k in ENGS[:4] + ['cc']:
            self.semh[k] = self.pstack.enter_context(nc.semaphore(k))
        self.cnt = {e: 0 for e in ENGS}
        self.cccnt = 0
        self.waited = {e: {} for e in ENGS}
        self.sem_pool = []
        self.ndma = 0
        self.ninst = 0
        self.stage_id = 0
        self._rank = {}
        self.begin_stage()

    def begin_stage(self):
        self.sstack = contextlib.ExitStack()
        self.ops = {e: [] for e in ENGS}
        self.stage_bufs = []
        self.stage_id += 1

    def sb(self, name, shape, dt):
        return self.sstack.enter_context(self.nc.sbuf_tensor('s%d_%s' % (self.stage_id, name), list(shape), dt))

    def ps(self, name, shape, dt):
        return self.sstack.enter_context(self.nc.psum_tensor('p%d_%s' % (self.stage_id, name), list(shape), dt))

    def rank(self, e, ename):
        if ename not in self._rank:
            self._rank[ename] = e.partition_id() % 4
        return self._rank[ename]

    def _deps(self, eng, reads, writes):
        need = {}
        for b in reads:
            if b.w is not None:
                k, v = b.w
                if need.get(k, 0) < v:
                    need[k] = v
        for b in writes:
            if b.w is not None:
                k, v = b.w
                if need.get(k, 0) < v:
                    need[k] = v
            for k, v in b.r.items():
                if need.get(k, 0) < v:
                    need[k] = v
        out = []
        wd = self.waited[eng]
        for k, v in need.items():
            if k == eng and (eng == 'pe' or not self.SAME_ENG_WAIT):
                continue
            if wd.get(k, 0) >= v:
                continue
            wd[k] = v
            out.append((k, v))
        return out

    def _mark(self, tok, reads, writes):
        k, v = tok
        for b in reads:
            if b.r.get(k, 0) < v:
                b.r[k] = v
        for b in writes:
            b.w = tok
            b.r = {}

    def op(self, eng, fns, reads=(), writes=()):
        if callable(fns):
            fns = [fns]
        waits = self._deps(eng, reads, writes)
        self.cnt[eng] += 1
        tok = (eng, self.cnt[eng])
        self.ops[eng].append((waits, fns, (eng, 1)))
        self._mark(tok, reads, writes)
        self.ninst += len(fns)
        return tok

    def _dma_sem(self, sembuf):
        if sembuf.semkey is None:
            if self.sem_pool:
                sembuf.semkey, sembuf.semcnt = self.sem_pool.pop()
            else:
                sembuf.semkey = 'dma%d' % self.ndma
                self.ndma += 1
                self.semh[sembuf.semkey] = self.pstack.enter_context(self.nc.semaphore(sembuf.semkey))
                sembuf.semcnt = 0
            self.stage_bufs.append(sembuf)

    def dma(self, queue, out_ap, in_ap, sembuf, reads=(), writes=(), is_output=False):
        waits = self._deps(queue, reads, writes)
        self._dma_sem(sembuf)
        sembuf.semcnt += 16
        tok = (sembuf.semkey, sembuf.semcnt)

        def fn(e, o=out_ap, i=in_ap, q=queue):
            oo = o(self.rank(e, q)) if callable(o) else o
            ii = i(self.rank(e, q)) if callable(i) else i
            return e.dma_start(out=oo, in_=ii)
        self.ops[queue].append((waits, [fn], (sembuf.semkey, 16)))
        self._mark(tok, reads, writes)
        self.ninst += 1
        return tok

    def coll(self, in_ap, out_ap, reads, writes, groups):
        waits = self._deps('pool', reads, writes)
        self.cccnt += 1
        tok = ('cc', self.cccnt)
        self.ops['pool'].append((waits, [lambda e: e.collective_compute(
            "AllGather", ALU.bypass, replica_groups=groups, ins=[in_ap], outs=[out_ap])], ('cc', 1)))
        self._mark(tok, reads, writes)
        return tok

    def end_stage(self):
        targets = {e: self.cnt[e] for e in ENGS[:4]}
        for b in self.stage_bufs:
            targets[b.semkey] = b.semcnt
        for e in ENGS:
            waits = []
            for k, v in targets.items():
                if k == e or v == 0:
                    continue
                if self.waited[e].get(k, 0) >= v:
                    continue
                self.waited[e][k] = v
                waits.append((k, v))
            self.ops[e].append((waits, [], None))
        semh = self.semh
        ops = self.ops

        def run(ename, e):
            for waits, fns, inc in ops[ename]:
                for k, v in waits:
                    e.wait_ge(semh[k], v)
                ins = None
                for fn in fns:
                    ins = fn(e)
                if inc is not None:
                    ins.then_inc(semh[inc[0]], inc[1])

        with self.nc.Block() as block:
            @block.sync
            def _(e):
                run('sp', e)

            @block.tensor
            def _(e):
                run('pe', e)

            @block.scalar
            def _(e):
                run('act', e)

            @block.vector
            def _(e):
                run('dve', e)

            @block.gpsimd
            def _(e):
                run('pool', e)
        for b in self.stage_bufs:
            self.sem_pool.append((b.semkey, b.semcnt))
            b.semkey = None
        self.sstack.close()
        self.begin_stage()

    def finish(self):
        self.sstack.close()
        self.pstack.close()


def _mm(out, lhsT, rhs, start, stop, skip=False):
    return lambda e: e.matmul(out, lhsT, rhs, start=start, stop=stop, skip_group_check=skip)


def _tp(out, in_, ident):
    return lambda e: e.transpose(out, in_, ident)


class Ring:
    def __init__(self, p, name, n, shape, dt, psum=False):
        self.t = [(p.ps if psum else p.sb)('%s%d' % (name, i), shape, dt) for i in range(n)]
        self.b = [Buf('%s%d' % (name, i)) for i in range(n)]
        self.i = 0
        self.n = n

    def next(self):
        i = self.i
        self.i = (i + 1) % self.n
        return self.t[i], self.b[i]


def _copy(p, eng, out, in_, reads, writes, scale=None):
    if eng == 'act':
        if scale is None:
            return p.op('act', lambda e: e.activation(out=out, in_=in_, func=AF.Copy), reads, writes)
        return p.op('act', lambda e: e.activation(out=out, in_=in_, func=AF.Copy, scale=float(scale)),
                    reads, writes)
    if scale is None:
        return p.op(eng, lambda e: e.tensor_copy(out, in_), reads, writes)
    return p.op(eng, lambda e: e.tensor_scalar_mul(out, in_, float(scale)), reads, writes)


class Norm:
    def __init__(self, p, ident, bident, tag):
        self.p = p
        self.ident, self.bident = ident, bident
        self.junk = Ring(p, tag + "junk", 2, [128, D], BF16)
        self.ss = Ring(p, tag + "ss", 2, [128, 1], F32)
        self.sd = Ring(p, tag + "sd", 2, [128, 1], F32)
        self.rs = Ring(p, tag + "rs", 2, [128, 1], F32)
        self.hn = Ring(p, tag + "hn", 2, [128, D], BF16)

    def rstd(self, x_ap, bx, eps_t, beps=None):
        p = self.p
        jk, bjk = self.junk.next()
        ss, bss = self.ss.next()
        sd, bsd = self.sd.next()
        rs, brs = self.rs.next()
        p.op('act', lambda e: e.activation(out=jk[:], in_=x_ap, func=AF.Square, accum_out=ss[:]),
             [bx], [bjk, bss])
        p.op('act', lambda e: e.activation(out=sd[:], in_=ss[:], func=AF.Sqrt, scale=1.0 / D,
                                           bias=eps_t[:]), [bss] + ([beps] if beps else []), [bsd])
        p.op('dve', lambda e: e.reciprocal(rs[:], sd[:]), [bsd], [brs])
        return rs, brs

    def run(self, x_ap, bx, g, bg, eps_t, tp_ps, btp, hnT_ap, bhnT, copy_eng, beps=None):
        p = self.p
        rs, brs = self.rstd(x_ap, bx, eps_t, beps)
        hn, bhn = self.hn.next()
        p.op('dve', lambda e: e.scalar_tensor_tensor(out=hn[:], in0=x_ap, scalar=rs[:, 0:1], in1=g[:],
                                                     op0=ALU.mult, op1=ALU.mult),
             [bx, brs, bg], [bhn])
        ident = self.ident
        p.op('pe', [_tp(tp_ps[:, kc, :], hn[:, kc * 128:(kc + 1) * 128], ident[:]) for kc in range(8)],
             [bhn, self.bident], [btp])
        _copy(p, copy_eng, hnT_ap, tp_ps[:, :, :], [btp], [bhnT])


WL = [8 * INW, 8 * 512, 10 * D, NJ * 2048, NJ * D]
WOFF = [0]
for _w in WL:
    WOFF.append(WOFF[-1] + _w)
LW = WOFF[-1]
TC = SEQ // 4
NST = TC // 512
GROUPS = [[0, 1, 2, 3], [4, 5, 6, 7]]
NEG = -200.0
GV_MIX, GV_FFN, GV_MEM, GV_FIN = 0, 4, 8, 9


class Dram:
    pass


def st_cast(p, dr):
    F = DEPTH * LW
    chunk = F // 128
    rin = Ring(p, "cin", 4, [128, chunk], F32)
    rout = Ring(p, "cout", 4, [128, chunk], BF16)
    engs = ['dve', 'act']
    for i in range(128):
        ti, bi = rin.next()
        to, bo = rout.next()
        p.dma('sp', ti[:], dr.wslab[:, i * chunk:(i + 1) * chunk], bi, writes=[bi])
        _copy(p, engs[i % 2], to[:], ti[:], [bi], [bo])
        p.dma('pool', dr.wbf[:, i * chunk:(i + 1) * chunk], to[:], bo, reads=[bo])
    p.end_stage()


def wsl(dr, l, i):
    o = l * LW
    return dr.wbf[:, o + WOFF[i]:o + WOFF[i + 1]]


def st_pre(p, dr, l, x_src):
    kind = l % 3
    NT, W = 4, 512
    w_in = wsl(dr, l, 0)
    w_kv = wsl(dr, l, 1)
    ident = p.sb("ident", [128, 128], BF16); bident = Buf("ident")
    g_sb = p.sb("g", [128, D], F32); bg = Buf("g")
    gmem_sb = p.sb("gmem", [128, D], F32); bgmem = Buf("gmem")
    eps_t = p.sb("eps", [128, 1], F32); beps = Buf("eps")
    w_sb = p.sb("w_in", [128, 8, INW], BF16)
    bw = [Buf("w_in%d" % kc) for kc in range(8)]
    wkv_sb = p.sb("w_kv", [128, 8, 512], BF16); bwkv = Buf("w_kv")
    p.dma('sp', ident[:], dr.ident, bident, writes=[bident])
    p.dma('sp', g_sb[:], dr.gvec[:, (GV_MIX + l) * D:(GV_MIX + l + 1) * D], bg, writes=[bg])
    p.dma('sp', gmem_sb[:], dr.gvec[:, GV_MEM * D:(GV_MEM + 1) * D], bgmem, writes=[bgmem])
    p.op('pool', lambda e: e.memset(eps_t[:], EPS), [], [beps])
    p.dma('sp', wkv_sb[:].rearrange("p k c -> p (k c)"), w_kv, bwkv, writes=[bwkv])
    for kc in range(8):
        p.dma('sp', w_sb[:, kc, :], w_in[:, kc * INW:(kc + 1) * INW], bw[kc], writes=[bw[kc]])
    if kind == 2:
        cw_sb = p.sb("cw", [128, 8, 3], F32); bcw = Buf("cw")
        p.dma('sp', cw_sb[:].rearrange("p c k -> p (c k)"), dr.cw, bcw, writes=[bcw])

    tp_ps = p.ps("tp", [128, 8, 128], BF16); btp = Buf("tp")
    acc = Ring(p, "acc", 4, [128, 512], F32, psum=True)
    sc_ps = Ring(p, "sc", 2, [128, 512], F32, psum=True)
    mo_ps = p.ps("mo", [128, 512], F32); bmo_ps = Buf("mo_ps")
    nrm = Norm(p, ident, bident, "n")
    memx = Ring(p, "memx", 2, [128, D], F32)
    memT = p.sb("memT", [128, 8, 256], BF16); bmemT = [Buf("memT0"), Buf("memT1")]
    kmT = p.sb("kmT", [128, 2, 256], BF16); bkmT = Buf("kmT")
    vm = p.sb("vm", [128, 2, 4, 65], BF16); bvm = Buf("vm")
    p.op('pool', lambda e: e.memset(vm[:].rearrange("p a h d -> p (a h d)"), 1.0), [], [bvm])
    for mc in range(2):
        t, b = memx.next()
        p.dma('sp', t[:], dr.mem[mc * 128:(mc + 1) * 128, :], b, writes=[b])
        nrm.run(t[:], b, gmem_sb, bgmem, eps_t, tp_ps, btp, memT[:, :, mc * 128:(mc + 1) * 128],
                bmemT[mc], 'act', beps)
    for cc in range(2):
        ps, bps = acc.next()
        p.op('pe', [_mm(ps[:, 0:256], wkv_sb[:, kc, cc * 128:(cc + 1) * 128], memT[:, kc, :],
                        kc == 0, kc == 7) for kc in range(8)], [bwkv] + bmemT, [bps])
        _copy(p, 'dve', kmT[:, cc, :], ps[:, 0:256], [bps], [bkmT])
    for mc in range(2):
        ps, bps = acc.next()
        p.op('pe', [_mm(ps[:, 0:256], memT[:, kc, mc * 128:(mc + 1) * 128], wkv_sb[:, kc, 256:512],
                        kc == 0, kc == 7) for kc in range(8)], [bwkv] + bmemT, [bps])
        _copy(p, 'act', vm[:, mc, :, 0:64], ps[:, 0:256].rearrange("p (h d) -> p h d", h=4), [bps], [bvm])

    xt = Ring(p, "xt", 1 if kind == 2 else 2, [128, NT, D], F32)
    hnT = Ring(p, "hnT", 2, [128, 8, W], BF16)
    stg = Ring(p, "stg", 8, [128, W], BF16)
    mqT = Ring(p, "mqT", 2, [128, 2, W], BF16)
    pT = Ring(p, "pT", 1 if kind == 2 else 2, [128, 4, 2, W], BF16)
    rsm = Ring(p, "rsm", 2, [128, 4], F32)
    mo_bf = Ring(p, "mobf", 2, [128, 256], BF16)
    moT_st = Ring(p, "moTst", 2, [128, 2, W], BF16)
    if kind == 2:
        gb = Ring(p, "gb", 2, [128, W], F32)
        gc = Ring(p, "gc", 2, [128, W], F32)
        uu = [p.sb("uu%d" % c, [128, W + 2], F32) for c in range(8)]
        buu = [Buf("uu%d" % c) for c in range(8)]
        yy = Ring(p, "yy", 2, [128, W], F32)
        hx = p.sb("hx", [128, 1, D], F32); bhx = Buf("hx")
        hnTh = p.sb("hnTh", [128, 8, 128], BF16); bhnTh = Buf("hnTh")
        hcr = Ring(p, "hc", 2, [128, D], F32)
        hsel = p.sb("hsel", [128, 4], F32); bhsel = Buf("hsel")
        p.dma('sp', hsel[:], dr.hsel, bhsel, writes=[bhsel])
    evac = ['act', 'dve']

    def proj_fm(chunk, hT, bhT, width):
        ps, bps = acc.next()
        p.op('pe', [_mm(ps[:, 0:width], w_sb[:, kc, chunk * 128:(chunk + 1) * 128], hT[:, kc, 0:width],
                        kc == 0, kc == 7) for kc in range(8)], bw + [bhT], [bps])
        return ps, bps

    for st in range(NST):
        t0 = st * W
        if kind == 2:
            pc, hf = st // 2, st % 2
            for i in range(4):
                ct, bct = hcr.next()
                if i < 3:
                    r0 = 128 + pc * 1024 + i * 256 + hf * 128
                    rb = [dr.b_hgath[pc]]
                elif st == 0:
                    r0, rb = 0, []
                else:
                    r0 = 128 + ((st - 1) // 2) * 1024 + 3 * 256 + ((st - 1) % 2) * 128
                    rb = [dr.b_hgath[(st - 1) // 2]]
                p.dma('sp', ct[:], dr.hgath[r0:r0 + 128, :], bct, reads=rb, writes=[bct])
                if i == 0:
                    p.op('dve', lambda e, ct=ct, i=i: e.tensor_scalar_mul(hx[:, 0, :], ct[:], hsel[:, i:i + 1]),
                         [bct, bhsel], [bhx])
                else:
                    p.op('dve', lambda e, ct=ct, i=i: e.scalar_tensor_tensor(
                        out=hx[:, 0, :], in0=ct[:], scalar=hsel[:, i:i + 1], in1=hx[:, 0, :],
                        op0=ALU.mult, op1=ALU.add), [bct, bhsel, bhx], [bhx])
            nrm.run(hx[:, 0, :], bhx, g_sb, bg, eps_t, tp_ps, btp, hnTh[:, :, :], bhnTh, 'act', beps)
            for c in range(8):
                psc, bpsc = proj_fm(8 + c, hnTh, bhnTh, 128)
                t, b = gc.next()
                _copy(p, 'act', t[:, 0:128], psc[:, 0:128], [bpsc], [b])
                psh, bpsh = proj_fm(16 + c, hnTh, bhnTh, 128)
                p.op('dve', lambda e, t=t, psh=psh, c=c: e.tensor_tensor(
                    out=uu[c][:, 0:2], in0=psh[:, 126:128], in1=t[:, 126:128], op=ALU.mult),
                    [bpsh, b], [buu[c]])
        xtile, bxt = xt.next()
        p.dma('sp', xtile[:], x_src[t0:t0 + W, :].rearrange("(j p) d -> p j d", p=128), bxt, writes=[bxt])
        hT, bhT = hnT.next()
        for j in range(NT):
            nrm.run(xtile[:, j, :], bxt, g_sb, bg, eps_t, tp_ps, btp, hT[:, :, j * 128:(j + 1) * 128],
                    bhT, evac[j % 2], beps)
        if kind == 2:
            for c in range(8):
                psc, bpsc = proj_fm(8 + c, hT, bhT, W)
                tgc, bgc = gc.next()
                _copy(p, 'act', tgc[:], psc[:, :], [bpsc], [bgc])
                psh, bpsh = proj_fm(16 + c, hT, bhT, W)
                p.op('dve', lambda e, c=c, psh=psh, tgc=tgc: e.tensor_tensor(
                    out=uu[c][:, 2:W + 2], in0=psh[:, :], in1=tgc[:], op=ALU.mult),
                    [bpsh, bgc], [buu[c]])
                psb, bpsb = proj_fm(c, hT, bhT, W)
                tgb, bgb = gb.next()
                _copy(p, 'act', tgb[:], psb[:, :], [bpsb], [bgb])
                ty, by = yy.next()
                p.op('act', lambda e, c=c, ty=ty: e.activation(out=ty[:], in_=uu[c][:, 0:W], func=AF.Copy,
                                                               scale=cw_sb[:, c, 0:1]), [buu[c], bcw], [by])
                p.op('dve', lambda e, c=c, ty=ty: e.scalar_tensor_tensor(
                    out=ty[:], in0=uu[c][:, 1:W + 1], scalar=cw_sb[:, c, 1:2], in1=ty[:],
                    op0=ALU.mult, op1=ALU.add), [buu[c], bcw, by], [by])
                p.op('dve', lambda e, c=c, ty=ty: e.scalar_tensor_tensor(
                    out=ty[:], in0=uu[c][:, 2:W + 2], scalar=cw_sb[:, c, 2:3], in1=ty[:],
                    op0=ALU.mult, op1=ALU.add), [buu[c], bcw, by], [by])
                so, bso = stg.next()
                p.op('pool', lambda e, so=so, ty=ty, tgb=tgb: e.tensor_tensor(
                    out=so[:], in0=ty[:], in1=tgb[:], op=ALU.mult), [by, bgb], [bso])
                p.dma('sp', dr.mixc[c * 128:(c + 1) * 128, t0:t0 + W], so[:], bso, reads=[bso])
        else:
            snd = dr.qkv_send[l]
            gth = dr.qkv_gath[l]
            for part in range(3):
                bsend = [Buf("snd") for _ in range(8)]
                for c8 in range(8):
                    c = part * 8 + c8
                    ps, bps = proj_fm(c, hT, bhT, W)
                    so, bso = stg.next()
                    _copy(p, evac[c % 2], so[:], ps[:, :], [bps], [bso], scale=(128.0 ** -0.5 if part == 0 else None))
                    r0 = st * 1024 + c8 * 128
                    p.dma('sp', snd[part][r0:r0 + 128, :], so[:], bso, reads=[bso], writes=[bsend[c8]])
                p.coll(snd[part][st * 1024:(st + 1) * 1024, :], gth[part][st],
                       reads=bsend, writes=[dr.b_qkv[l][part][st]], groups=GROUPS)
        mq, bmq = mqT.next()
        for cc in range(2):
            ps, bps = proj_fm(24 + cc, hT, bhT, W)
            _copy(p, evac[cc], mq[:, cc, :], ps[:, :], [bps], [bmq])
        pt, bpt = pT.next()
        for h in range(4):
            cc, po = h // 2, (h % 2) * 64
            for mc in range(2):
                ps, bps = sc_ps.next()
                p.op('pe', [_mm(ps[:, :], kmT[po:po + 64, cc, mc * 128:(mc + 1) * 128],
                                mq[po:po + 64, cc, :], True, True)], [bkmT, bmq], [bps])
                p.op('act', lambda e, ps=ps, h=h, mc=mc, pt=pt: e.activation(
                    out=pt[:, h, mc, :], in_=ps[:, :], func=AF.Exp, scale=0.125), [bps], [bpt])
        mst, bmst = moT_st.next()
        for j in range(NT):
            mms = []
            for h in range(4):
                for mc in range(2):
                    mms.append(_mm(mo_ps[:, h * 65:(h + 1) * 65], pt[:, h, mc, j * 128:(j + 1) * 128],
                                   vm[:, mc, h, :], (h == 0 and mc == 0), (h == 3 and mc == 1), skip=True))
            p.op('pe', mms, [bpt, bvm], [bmo_ps])
            rs, brs = rsm.next()
            mo3 = mo_ps[:, 0:260].rearrange("p (h d) -> p h d", h=4)
            p.op('dve', lambda e, rs=rs, mo3=mo3: e.reciprocal(rs[:], mo3[:, :, 64]), [bmo_ps], [brs])
            mb, bmb = mo_bf.next()
            for h in range(4):
                p.op('dve', lambda e, h=h, mb=mb, rs=rs: e.tensor_scalar_mul(
                    mb[:, h * 64:(h + 1) * 64], mo_ps[:, h * 65:h * 65 + 64], rs[:, h:h + 1]),
                    [bmo_ps, brs], [bmb])
            p.op('pe', [_tp(tp_ps[:, cc, :], mb[:, cc * 128:(cc + 1) * 128], ident[:]) for cc in range(2)],
                 [bmb, bident], [btp])
            _copy(p, 'act', mst[:, :, j * 128:(j + 1) * 128], tp_ps[:, 0:2, :], [btp], [bmst])
        for cc in range(2):
            p.dma('sp', dr.moT[cc * 128:(cc + 1) * 128, t0:t0 + W], mst[:, cc, :], bmst, reads=[bmst])
    p.end_stage()


def st_post(p, dr, l, x_src, x_dst):
    final = (l == DEPTH - 1)
    kind = l % 3
    NT, W = 4, 512
    w_o, w_fi, w_fo = wsl(dr, l, 2), wsl(dr, l, 3), wsl(dr, l, 4)
    ident = p.sb("ident", [128, 128], BF16); bident = Buf("ident")
    g_sb = p.sb("g", [128, D], F32); bg = Buf("g")
    eps_t = p.sb("eps", [128, 1], F32); beps = Buf("eps")
    wo_sb = p.sb("w_o", [128, 10, D], BF16); bwo = Buf("w_o")
    wfo_sb = p.sb("w_fo", [128, NJ, D], BF16)
    bwfo = [Buf("w_fo%d" % i) for i in range(2)]
    p.dma('sp', ident[:], dr.ident, bident, writes=[bident])
    p.dma('sp', g_sb[:], dr.gvec[:, (GV_FFN + l) * D:(GV_FFN + l + 1) * D], bg, writes=[bg])
    p.op('pool', lambda e: e.memset(eps_t[:], EPS), [], [beps])
    p.dma('sp', wo_sb[:].rearrange("p k c -> p (k c)"), w_o, bwo, writes=[bwo])
    for i in range(2):
        p.dma('sp', wfo_sb[:, i * 11:(i + 1) * 11, :].rearrange("p k c -> p (k c)"),
              w_fo[:, i * 11 * D:(i + 1) * 11 * D], bwfo[i], writes=[bwfo[i]])
    if final:
        gfin_sb = p.sb("gfin", [128, D], F32); bgfin = Buf("gfin")
        p.dma('sp', gfin_sb[:], dr.gvec[:, GV_FIN * D:(GV_FIN + 1) * D], bgfin, writes=[bgfin])
        ofin = Ring(p, "ofin", 2, [128, D], F32)
    tp_ps = p.ps("tp", [128, 8, 128], BF16); btp = Buf("tp")
    acc = Ring(p, "acc", 3, [128, 512], F32, psum=True)
    gu = Ring(p, "gu", 4, [128, 512], F32, psum=True)
    nrm = Norm(p, ident, bident, "n")
    xt = Ring(p, "xt", 2, [128, NT, D], F32)
    mcT = Ring(p, "mcT", 1, [128, 10, W], BF16)
    hnT = Ring(p, "hnT", 1, [128, 8, W], BF16)
    hT = Ring(p, "hT", 1, [128, NJ, W], BF16)
    wslr = Ring(p, "wsl", 4, [128, 8, 2, 128], BF16)
    sg = Ring(p, "sg", 2, [128, W], F32)
    evac = ['act', 'dve']
    halo_bufs = []
    for st in range(NST):
        t0 = st * W
        xtile, bxt = xt.next()
        p.dma('sp', xtile[:], x_src[t0:t0 + W, :].rearrange("(j p) d -> p j d", p=128), bxt, writes=[bxt])
        mc, bmc = mcT.next()
        if kind == 2:
            p.dma('sp', mc[:, 0:8, :], dr.mixc[:, t0:t0 + W].rearrange("(k p) t -> p k t", p=128), bmc, writes=[bmc])
        else:
            og = dr.o_gath[l]
            p.dma('sp', mc[:, 0:8, :],
                  lambda r, st=st, og=og: og[st].rearrange("(k p) t -> p k t", p=128)[:, :, bass.ds(r * 512, 512)],
                  bmc, reads=[dr.b_o[l][st]], writes=[bmc])
        p.dma('sp', mc[:, 8:10, :], dr.moT[:, t0:t0 + W].rearrange("(k p) t -> p k t", p=128), bmc, writes=[bmc])
        hn, bhn = hnT.next()
        for j in range(NT):
            for g in range(2):
                ps, bps = acc.next()
                p.op('pe', [_mm(ps[:, :], mc[:, kc, j * 128:(j + 1) * 128], wo_sb[:, kc, g * 512:(g + 1) * 512],
                                kc == 0, kc == 9) for kc in range(10)], [bmc, bwo], [bps])
                p.op('dve', lambda e, ps=ps, j=j, g=g, xtile=xtile: e.tensor_tensor(
                    out=xtile[:, j, g * 512:(g + 1) * 512], in0=ps[:, :], in1=xtile[:, j, g * 512:(g + 1) * 512],
                    op=ALU.add), [bps, bxt], [bxt])
            nrm.run(xtile[:, j, :], bxt, g_sb, bg, eps_t, tp_ps, btp, hn[:, :, j * 128:(j + 1) * 128],
                    bhn, evac[j % 2], beps)
        h, bh = hT.next()
        for jj in range(NJ):
            ws, bws = wslr.next()
            p.dma('sp', ws[:].rearrange("p k s c -> p (k s c)"), w_fi[:, jj * 2048:(jj + 1) * 2048], bws, writes=[bws])
            psg, bpsg = gu.next()
            p.op('pe', [_mm(psg[:, :], ws[:, kc, 0, :], hn[:, kc, :], kc == 0, kc == 7) for kc in range(8)],
                 [bws, bhn], [bpsg])
            psu, bpsu = gu.next()
            p.op('pe', [_mm(psu[:, :], ws[:, kc, 1, :], hn[:, kc, :], kc == 0, kc == 7) for kc in range(8)],
                 [bws, bhn], [bpsu])
            s, bs = sg.next()
            p.op('act', lambda e, s=s, psg=psg: e.activation(out=s[:], in_=psg[:, :], func=AF.Silu), [bpsg], [bs])
            p.op('dve', lambda e, s=s, psu=psu, jj=jj, h=h: e.tensor_tensor(
                out=h[:, jj, :], in0=psu[:, :], in1=s[:], op=ALU.mult), [bpsu, bs], [bh])
        for j in range(NT):
            for g in range(2):
                ps, bps = acc.next()
                p.op('pe', [_mm(ps[:, :], h[:, jj, j * 128:(j + 1) * 128], wfo_sb[:, jj, g * 512:(g + 1) * 512],
                                jj == 0, jj == NJ - 1) for jj in range(NJ)], [bh] + bwfo, [bps])
                p.op('dve', lambda e, ps=ps, j=j, g=g, xtile=xtile: e.tensor_tensor(
                    out=xtile[:, j, g * 512:(g + 1) * 512], in0=ps[:, :], in1=xtile[:, j, g * 512:(g + 1) * 512],
                    op=ALU.add), [bps, bxt], [bxt])
            if final:
                rs, brs = nrm.rstd(xtile[:, j, :], bxt, eps_t, beps)
                o, bo = ofin.next()
                p.op('dve', lambda e, o=o, j=j, xtile=xtile, rs=rs: e.scalar_tensor_tensor(
                    out=o[:], in0=xtile[:, j, :], scalar=rs[:, 0:1], in1=gfin_sb[:], op0=ALU.mult, op1=ALU.mult),
                    [bxt, brs, bgfin], [bo])
                p.dma('sp', x_dst[t0 + j * 128:t0 + (j + 1) * 128, :], o[:], bo, reads=[bo])
        if not final:
            p.dma('sp', x_dst[t0:t0 + W, :].rearrange("(j p) d -> p j d", p=128), xtile[:], bxt, reads=[bxt])
        if l % 3 == 1 and l + 1 < DEPTH:
            bh_ = Buf("hs")
            p.dma('sp', dr.hsend[st * 128:(st + 1) * 128, :], xtile[:, 3, :], bxt, reads=[bxt], writes=[bh_])
            halo_bufs.append(bh_)
            if st % 2 == 1:
                pc = st // 2
                p.coll(dr.hsend[pc * 256:(pc + 1) * 256, :], dr.hgath[128 + pc * 1024:128 + (pc + 1) * 1024, :],
                       reads=halo_bufs[-2:], writes=[dr.b_hgath[pc]], groups=GROUPS)
    p.end_stage()


def _load_kv_piece(p, dr, l, part, st, off_fn, out_ap3, sembuf, writes):
    gth = dr.qkv_gath[l][part][st]

    def src(r, gth=gth):
        v3 = gth.rearrange("(r f) t -> f r t", r=4)
        return v3[bass.ds(off_fn(r), 128), :, :]
    p.dma('sp', out_ap3, src, sembuf, reads=[dr.b_qkv[l][part][st]], writes=writes)


def st_sb(p, dr, l):
    S, NH = SEQ, 2
    tri = p.sb("tri", [128, 128], BF16); btri = Buf("tri")
    ones = p.sb("ones", [128, 128], BF16); bones = Buf("ones")
    ident = p.sb("ident", [128, 128], BF16); bident = Buf("ident")
    msk = p.sb("msk", [128, 896], F32); bmsk = Buf("msk")
    one1 = p.sb("one1", [128, 1], F32); bone1 = Buf("one1")
    p.dma('sp', tri[:], dr.tri, btri, writes=[btri])
    p.dma('sp', ones[:], dr.ones, bones, writes=[bones])
    p.dma('sp', ident[:], dr.ident, bident, writes=[bident])
    p.dma('sp', msk[:], dr.msk, bmsk, writes=[bmsk])
    p.op('pool', lambda e: e.memset(one1[:], 1.0), [], [bone1])
    NG = S // 512
    kring = Ring(p, "kT", 2, [128, S], BF16)
    vring = Ring(p, "v", 2, [128, S], BF16)
    vtr = Ring(p, "vTp", 2, [128, 2048], BF16)
    qring = Ring(p, "qg", 2, [128, 2048], BF16)
    zps = Ring(p, "z", 2, [128, 512], F32, psum=True)
    accr = Ring(p, "acc", 2, [128, 512], F32, psum=True)
    ops_ = Ring(p, "o", 3, [128, 512], F32, psum=True)
    tp_ps = p.ps("tp", [128, 8, 128], BF16); btp = Buf("tp")
    er = Ring(p, "e", 5, [128, 512], F32)
    ur = Ring(p, "u", 4, [128, 512], BF16)
    ar = Ring(p, "a", 3, [128, 512], BF16)
    gr = Ring(p, "g", 4, [128, 512], F32)
    osb = Ring(p, "osb", 2, [128, 512], BF16)
    su = p.sb("su", [128, 128], BF16); bsu = Buf("su")
    p.op('dve', lambda e: e.tensor_tensor(out=su[:], in0=ones[:], in1=tri[:], op=ALU.subtract), [bones, btri], [bsu])
    heads = {}

    kpb = {}
    vpb = {}

    def alloc_head(h):
        kt, _ = kring.next()
        vt, _ = vring.next()
        hi = kring.i
        if hi not in kpb:
            kpb[hi] = [Buf("kp") for _ in range(NST)]
            vpb[hi] = [Buf("vp") for _ in range(NST)]
        heads[h] = (kt, kpb[hi], vt, vpb[hi])

    def load_kv(h, st):
        kt, bkt, vt, bvt = heads[h]
        _load_kv_piece(p, dr, l, 1, st, lambda r, h=h: r * 256 + 128 * h,
                       kt[:, st * 2048:(st + 1) * 2048].rearrange("p (r t) -> p r t", r=4), bkt[st], [bkt[st]])
        vp, bvp = vtr.next()
        _load_kv_piece(p, dr, l, 2, st, lambda r, h=h: r * 256 + 128 * h,
                       vp[:, :].rearrange("p (r t) -> p r t", r=4), bvp, [bvp])
        for half in range(2):
            p.op('pe', [_tp(tp_ps[:, i, :], vp[:, (half * 8 + i) * 128:(half * 8 + i + 1) * 128], ident[:])
                        for i in range(8)], [bvp, bident], [btp])
            c0 = st * 16 + half * 8
            _copy(p, 'dve', vt[:, c0 * 128:(c0 + 8) * 128].rearrange("p (c d) -> p c d", c=8),
                  tp_ps[:, :, :], [btp], [bvt[st]])

    def load_head(h):
        alloc_head(h)
        for st in range(NST):
            load_kv(h, st)

    pieces = {}
    porder = [(h, st) for h in range(NH) for st in range(NST)]

    def load_qpiece(h, st):
        qg, bqg = qring.next()
        _load_kv_piece(p, dr, l, 0, st, lambda r, h=h: r * 256 + 128 * h,
                       qg[:, :].rearrange("p (r t) -> p r t", r=4), bqg, [bqg])
        pieces[(h, st)] = (qg, bqg)

    steps = []
    for h in range(NH):
        for G in range(NG):
            cs = list(range(4 * G + 3, -1, -1))
            for i, c in enumerate(cs):
                steps.append(dict(h=h, G=G, c=c, first=(i == 0), last=(c == 0), pd=c - 4 * G))
    order = [(h, G) for h in range(NH) for G in range(NG)]
    gidx = {hg: i for i, hg in enumerate(order)}
    alloc_head(0)
    load_kv(0, 0)
    load_qpiece(0, 0)
    gstate = {}
    obufs = {}

    def s12(s):
        h, G, c = s['h'], s['G'], s['c']
        if s['first']:
            if G % 4 == 0:
                if h == 0 and G // 4 + 1 < NST:
                    load_kv(0, G // 4 + 1)
                pi = porder.index((h, G // 4))
                if pi + 1 < len(porder):
                    load_qpiece(*porder[pi + 1])
            if G == NG - 8 and h + 1 < NH:
                load_head(h + 1)
            acc, bacc = accr.next()
            ot, bot = ops_.next()
            gstate[(h, G)] = dict(acc=acc, bacc=bacc, ot=ot, bot=bot)
        kt, bkt, vt, bvt = heads[h]
        qg, bqg = pieces[(h, G // 4)]
        qc0 = (G % 4) * 512
        qlo = 128 * s['pd'] if s['pd'] > 0 else 0
        s['qlo'] = qlo
        z, bz = zps.next()
        p.op('pe', [_mm(z[:, qlo:512], kt[:, c * 128:(c + 1) * 128], qg[:, qc0 + qlo:qc0 + 512], True, True)],
             [bkt[c // 16], bqg], [bz])
        e_, be = er.next()
        p.op('act', lambda e, e_=e_, z=z, qlo=qlo: e.activation(out=e_[:, qlo:512], in_=z[:, qlo:512], func=AF.Exp),
             [bz], [be])
        if s['pd'] >= 0:
            off = 384 - 128 * s['pd']
            p.op('dve', lambda e, e_=e_, qlo=qlo, off=off: e.tensor_tensor(
                out=e_[:, qlo:512], in0=e_[:, qlo:512], in1=msk[:, off + qlo:off + 512], op=ALU.mult),
                [be, bmsk], [be])
        s.update(e=e_, be=be)

    def sLN(s):
        qlo = s['qlo']
        e_, be = s['e'], s['be']
        u_, bu = ur.next()
        p.op('act', lambda e, u_=u_, e_=e_, qlo=qlo: e.activation(out=u_[:, qlo:512], in_=e_[:, qlo:512], func=AF.Ln,
                                                                 bias=1.0, scale=1.0), [be], [bu])
        s.update(u=u_, bu=bu)

    def s34(s):
        h, G, c = s['h'], s['G'], s['c']
        gs = gstate[(h, G)]
        qlo = s['qlo']
        acc, bacc = gs['acc'], gs['bacc']
        u_, bu = s['u'], s['bu']
        p.op('pe', [_mm(acc[:, qlo:512], tri[:, :], u_[:, qlo:512], s['first'], s['last'], skip=True)],
             [btri, bu], [bacc])
        g_, bg_ = gr.next()
        p.op('dve', lambda e, g_=g_, acc=acc, qlo=qlo: e.tensor_copy(g_[:, qlo:512], acc[:, qlo:512]), [bacc], [bg_])
        p.op('act', lambda e, g_=g_, qlo=qlo: e.activation(out=g_[:, qlo:512], in_=g_[:, qlo:512],
                                                           func=AF.Exp, scale=-1.0), [bg_], [bg_])
        s.update(g=g_, bg=bg_)

    def s4(s):
        qlo = s['qlo']
        g_, bg_ = s['g'], s['bg']
        a_, ba = ar.next()
        e_, be = s['e'], s['be']
        p.op('dve', lambda e, a_=a_, e_=e_, g_=g_, qlo=qlo: e.tensor_tensor(
            out=a_[:, qlo:512], in0=g_[:, qlo:512], in1=e_[:, qlo:512], op=ALU.mult), [be, bg_], [ba])
        s.update(a=a_, ba=ba)

    def sSU(s):
        if s['last']:
            return
        gs = gstate[(s['h'], s['G'])]
        qlo = s['qlo']
        acc, bacc = gs['acc'], gs['bacc']
        p.op('pe', [_mm(acc[:, qlo:512], su[:, :], s['u'][:, qlo:512], False, False, skip=True)],
             [bsu, s['bu']], [bacc])

    def s5(s):
        h, G, c = s['h'], s['G'], s['c']
        gs = gstate[(h, G)]
        kt, bkt, vt, bvt = heads[h]
        qlo = s['qlo']
        ot, bot = gs['ot'], gs['bot']
        p.op('pe', [_mm(ot[:, qlo:512], vt[:, c * 128:(c + 1) * 128], s['a'][:, qlo:512], s['first'], s['last'],
                        skip=True)], [bvt[c // 16], s['ba']], [bot])
        if s['last']:
            o, bo = osb.next()
            _copy(p, 'dve', o[:], ot[:, :], [bot], [bo])
            st, rr = G // 4, G % 4
            bsnd = Buf("osnd")
            obufs[(h, G)] = bsnd
            p.dma('sp', dr.o_send[l][st * 256 + h * 128:st * 256 + (h + 1) * 128, rr * 512:(rr + 1) * 512], o[:], bo,
                  reads=[bo], writes=[bsnd])
            if h == NH - 1 and rr == 3:
                p.coll(dr.o_send[l][st * 256:(st + 1) * 256, :], dr.o_gath[l][st],
                       reads=[obufs[(hh, 4 * st + r_)] for hh in range(NH) for r_ in range(4)],
                       writes=[dr.b_o[l][st]], groups=GROUPS)

    n = len(steps)
    for i in range(-1, n + 4):
        if 0 <= i + 1 < n:
            s12(steps[i + 1])
        if 0 <= i < n:
            sLN(steps[i])
        if 0 <= i - 2 < n:
            sSU(steps[i - 2])
        if 0 <= i - 1 < n:
            s34(steps[i - 1])
        if 0 <= i - 2 < n:
            s4(steps[i - 2])
        if 0 <= i - 3 < n:
            s5(steps[i - 3])
    p.end_stage()


def st_diff(p, dr, l):
    S = SEQ
    NC, NG = S // 128, S // 512
    lambda_init = 0.8 - 0.6 * math.exp(-0.3 * l)
    tab = p.sb("tab", [128, 1024], F32); btab = Buf("tab")
    biasT = p.sb("biasT", [128, NC + 1], F32); bbias = Buf("biasT")
    ones = p.sb("ones", [128, 128], BF16); bones = Buf("ones")
    ident = p.sb("ident", [128, 128], BF16); bident = Buf("ident")
    lamv = p.sb("lamv", [128, 4, 128], F32); blamv = Buf("lamv")
    gh = p.sb("gh", [128, 256], F32); bgh = Buf("gh")
    p.dma('sp', tab[:], dr.tab, btab, writes=[btab])
    p.dma('sp', biasT[:], dr.biasT, bbias, writes=[bbias])
    p.dma('sp', ones[:], dr.ones, bones, writes=[bones])
    p.dma('sp', ident[:], dr.ident, bident, writes=[bident])
    p.dma('sp', lamv[:].rearrange("p a d -> p (a d)"), dr.lamv, blamv, writes=[blamv])
    p.dma('sp', gh[:], dr.gh, bgh, writes=[bgh])
    k1 = p.sb("k1T", [128, S], BF16); bk1 = [Buf("k1") for _ in range(NST)]
    k2 = p.sb("k2T", [128, S], BF16); bk2 = [Buf("k2") for _ in range(NST)]
    vs = p.sb("v", [128, NC, 256], BF16); bv = [Buf("v") for _ in range(NST)]
    vtr = Ring(p, "vTp", 2, [128, 2048], BF16)
    zps = Ring(p, "z", 2, [128, 512], F32, psum=True)
    tp_ps = p.ps("tp", [128, 8, 128], BF16); btp = Buf("tp")
    obank = [p.ps("ob%d" % j, [128, 512], F32) for j in range(4)]
    bob = [Buf("ob%d" % j) for j in range(4)]
    sbank = p.ps("sums", [128, 512], F32); bsb = Buf("sums")
    def load_kv(st):
        _load_kv_piece(p, dr, l, 1, st, lambda r: r * 256,
                       k1[:, st * 2048:(st + 1) * 2048].rearrange("p (r t) -> p r t", r=4), bk1[st], [bk1[st]])
        _load_kv_piece(p, dr, l, 1, st, lambda r: r * 256 + 128,
                       k2[:, st * 2048:(st + 1) * 2048].rearrange("p (r t) -> p r t", r=4), bk2[st], [bk2[st]])
        for i in range(2):
            vp, bvp = vtr.next()
            _load_kv_piece(p, dr, l, 2, st, lambda r, i=i: r * 256 + 128 * i,
                           vp[:, :].rearrange("p (r t) -> p r t", r=4), bvp, [bvp])
            for half in range(2):
                p.op('pe', [_tp(tp_ps[:, j, :], vp[:, (half * 8 + j) * 128:(half * 8 + j + 1) * 128], ident[:])
                            for j in range(8)], [bvp, bident], [btp])
                c0 = st * 16 + half * 8
                _copy(p, 'dve' if half else 'act', vs[:, c0:c0 + 8, i * 128:(i + 1) * 128], tp_ps[:, :, :], [btp],
                      [bv[st]])

    prod = p.sb("prod", [128, 128], F32); bprod = Buf("prod")
    sl = p.sb("sl", [128, 2], F32); bsl = Buf("sl")
    el = p.sb("el", [128, 2], F32); bel = Buf("el")
    lam = p.sb("lam", [128, 1], F32); blam = Buf("lam")
    eps_t = p.sb("eps", [128, 1], F32); beps = Buf("eps")
    p.op('pool', lambda e: e.memset(eps_t[:], HEPS), [], [beps])
    for i in range(2):
        p.op('dve', lambda e, i=i: e.tensor_tensor(out=prod[:], in0=lamv[:, 2 * i, :], in1=lamv[:, 2 * i + 1, :],
                                                   op=ALU.mult), [blamv], [bprod])
        p.op('dve', lambda e, i=i: e.reduce_sum(out=sl[:, i:i + 1], in_=prod[:], axis=mybir.AxisListType.X),
             [bprod], [bsl])
    p.op('act', lambda e: e.activation(out=el[:], in_=sl[:], func=AF.Exp), [bsl], [bel])
    p.op('dve', lambda e: e.tensor_tensor(out=lam[:], in0=el[:, 0:1], in1=el[:, 1:2], op=ALU.subtract), [bel], [blam])
    p.op('dve', lambda e: e.tensor_scalar_add(lam[:], lam[:], float(lambda_init)), [blam], [blam])
    p.op('dve', lambda e: e.tensor_scalar_mul(gh[:], gh[:], float(1.0 - lambda_init)), [bgh], [bgh])

    q1r = Ring(p, "q1g", 2, [128, 2048], BF16)
    q2r = Ring(p, "q2g", 2, [128, 2048], BF16)
    tr = Ring(p, "t", 4, [128, 512], F32)
    Er = Ring(p, "E", 6, [128, 512], BF16)
    rsr = Ring(p, "rs", 2, [128, 2], F32)
    tmpr = Ring(p, "tmp", 2, [128, 256], F32)
    ofr = Ring(p, "of", 2, [128, 256], F32)
    jkr = Ring(p, "jk", 2, [128, 256], BF16)
    ssr = Ring(p, "ss", 2, [128, 1], F32)
    sdr = Ring(p, "sd", 2, [128, 1], F32)
    rdr = Ring(p, "rd", 2, [128, 1], F32)
    obr = Ring(p, "obf", 2, [128, 256], BF16)
    ostg = Ring(p, "ostg", 2, [128, 2, 512], BF16)
    pieces = {}

    def load_qpiece(st):
        a, ba = q1r.next()
        b, bb = q2r.next()
        _load_kv_piece(p, dr, l, 0, st, lambda r: r * 256, a[:, :].rearrange("p (r t) -> p r t", r=4), ba, [ba])
        _load_kv_piece(p, dr, l, 0, st, lambda r: r * 256 + 128, b[:, :].rearrange("p (r t) -> p r t", r=4), bb, [bb])
        pieces[st] = (a, ba, b, bb)

    steps = []
    for G in range(NG):
        cs = list(range(4 * G + 3, -1, -1))
        for i, c in enumerate(cs):
            steps.append(dict(G=G, c=c, first=(i == 0), last=(c == 0), pd=c - 4 * G))
    load_kv(0)
    load_qpiece(0)
    obufs = {}

    def s1(s):
        G, c, pd = s['G'], s['c'], s['pd']
        if s['first'] and G % 4 == 0 and G // 4 + 1 < NST:
            load_kv(G // 4 + 1)
            load_qpiece(G // 4 + 1)
        qa, bqa, qb, bqb = pieces[G // 4]
        qc0 = (G % 4) * 512
        qlo = 128 * pd if pd > 0 else 0
        s['qlo'] = qlo
        if pd >= 0:
            off, bi = 384 - 128 * pd, 0
        else:
            off, bi = 512, (-pd) - 1
        Es = []
        for (kk, bkk, qq, bqq) in ((k1, bk1, qa, bqa), (k2, bk2, qb, bqb)):
            z, bz = zps.next()
            p.op('pe', [_mm(z[:, qlo:512], kk[:, c * 128:(c + 1) * 128], qq[:, qc0 + qlo:qc0 + 512], True, True)],
                 [bkk[c // 16], bqq], [bz])
            t, bt = tr.next()
            p.op('dve', lambda e, t=t, z=z, qlo=qlo, off=off: e.tensor_tensor(
                out=t[:, qlo:512], in0=z[:, qlo:512], in1=tab[:, off + qlo:off + 512], op=ALU.add),
                [bz, btab], [bt])
            E, bE = Er.next()
            p.op('act', lambda e, E=E, t=t, qlo=qlo, bi=bi: e.activation(
                out=E[:, qlo:512], in_=t[:, qlo:512], func=AF.Exp, bias=biasT[:, bi:bi + 1], scale=1.0),
                [bt, bbias], [bE])
            Es.append((E, bE))
        s['E'] = Es

    def s2(s):
        G, c, pd = s['G'], s['c'], s['pd']
        (E1, bE1), (E2, bE2) = s['E']
        j0 = pd if pd > 0 else 0
        for jq in range(j0, 4):
            startj = (pd == jq)
            mms = [
                _mm(obank[jq][:, 0:256], E1[:, jq * 128:(jq + 1) * 128], vs[:, c, :], startj, False, skip=True),
                _mm(obank[jq][:, 256:512], E2[:, jq * 128:(jq + 1) * 128], vs[:, c, :], False, s['last'], skip=True),
                _mm(sbank[:, 2 * jq:2 * jq + 1], E1[:, jq * 128:(jq + 1) * 128], ones[:, 0:1],
                    (pd == 3 and jq == 3), False, skip=True),
                _mm(sbank[:, 2 * jq + 1:2 * jq + 2], E2[:, jq * 128:(jq + 1) * 128], ones[:, 0:1],
                    False, s['last'], skip=True),
            ]
            p.op('pe', mms, [bE1, bE2, bv[c // 16], bones], [bob[jq], bsb])
        if s['last']:
            og, bog = ostg.next()
            for jq in range(4):
                rs, brs = rsr.next()
                p.op('dve', lambda e, rs=rs, jq=jq: e.reciprocal(rs[:], sbank[:, 2 * jq:2 * jq + 2]), [bsb], [brs])
                p.op('dve', lambda e, rs=rs: e.tensor_tensor(out=rs[:, 1:2], in0=rs[:, 1:2], in1=lam[:, 0:1],
                                                            op=ALU.mult), [brs, blam], [brs])
                tmp, btmp = tmpr.next()
                p.op('act', lambda e, tmp=tmp, jq=jq, rs=rs: e.activation(
                    out=tmp[:], in_=obank[jq][:, 256:512], func=AF.Copy, scale=rs[:, 1:2]), [bob[jq], brs], [btmp])
                of, bof = ofr.next()
                p.op('dve', lambda e, of=of, jq=jq, rs=rs, tmp=tmp: e.scalar_tensor_tensor(
                    out=of[:], in0=obank[jq][:, 0:256], scalar=rs[:, 0:1], in1=tmp[:], op0=ALU.mult,
                    op1=ALU.subtract), [bob[jq], brs, btmp], [bof])
                jk, bjk = jkr.next()
                ss, bss = ssr.next()
                sd, bsd = sdr.next()
                rd, brd = rdr.next()
                p.op('act', lambda e, jk=jk, of=of, ss=ss: e.activation(out=jk[:], in_=of[:], func=AF.Square,
                                                                        accum_out=ss[:]), [bof], [bjk, bss])
                p.op('act', lambda e, sd=sd, ss=ss: e.activation(out=sd[:], in_=ss[:], func=AF.Sqrt, scale=1.0 / 256,
                                                                 bias=eps_t[:]), [bss, beps], [bsd])
                p.op('dve', lambda e, rd=rd, sd=sd: e.reciprocal(rd[:], sd[:]), [bsd], [brd])
                ob, bobf = obr.next()
                p.op('dve', lambda e, ob=ob, of=of, rd=rd: e.scalar_tensor_tensor(
                    out=ob[:], in0=of[:], scalar=rd[:, 0:1], in1=gh[:], op0=ALU.mult, op1=ALU.mult),
                    [bof, brd, bgh], [bobf])
                p.op('pe', [_tp(tp_ps[:, i, :], ob[:, i * 128:(i + 1) * 128], ident[:]) for i in range(2)],
                     [bobf, bident], [btp])
                _copy(p, 'act', og[:, :, jq * 128:(jq + 1) * 128], tp_ps[:, 0:2, :], [btp], [bog])
            st, rr = G // 4, G % 4
            bsnd = Buf("osnd")
            obufs[G] = bsnd
            p.dma('sp', dr.o_send[l][st * 256:(st + 1) * 256, rr * 512:(rr + 1) * 512].rearrange(
                "(i p) t -> p i t", p=128), og[:], bog, reads=[bog], writes=[bsnd])
            if rr == 3:
                p.coll(dr.o_send[l][st * 256:(st + 1) * 256, :], dr.o_gath[l][st],
                       reads=[obufs[4 * st + r_] for r_ in range(4)], writes=[dr.b_o[l][st]], groups=GROUPS)

    n = len(steps)
    for i in range(n + 1):
        if i < n:
            s1(steps[i])
        if 0 <= i - 1 < n:
            s2(steps[i - 1])
    p.end_stage()


ATT_LAYERS = [l for l in range(DEPTH) if l % 3 != 2]


def build_fused():
    nc = bass.Bass("TRN2", target_bir_lowering=False)
    dt = nc.dram_tensor
    dr = Dram()

    def ext(name, shape, dtype):
        return dt(name, list(shape), dtype, kind="ExternalInput").ap()
    dr.x = ext("x", [TC, D], F32)
    dr.wslab = ext("wslab", [128, DEPTH * LW], F32)
    dr.mem = ext("mem", [256, D], F32)
    dr.gvec = ext("gvec", [128, 10 * D], F32)
    dr.ident = ext("ident", [128, 128], BF16)
    dr.tri = ext("tri", [128, 128], BF16)
    dr.ones = ext("ones", [128, 128], BF16)
    dr.msk = ext("msk", [128, 896], F32)
    dr.tab = ext("tab", [128, 1024], F32)
    dr.biasT = ext("biasT", [128, SEQ // 128 + 1], F32)
    dr.lamv = ext("lamv", [128, 512], F32)
    dr.gh = ext("gh", [128, 256], F32)
    dr.cw = ext("cw", [128, 24], F32)
    dr.hsel = ext("hsel", [128, 4], F32)
    dr.y = dt("y", [TC, D], F32, kind="ExternalOutput").ap()
    dr.wbf = dt("wbf", [128, DEPTH * LW], BF16).ap()
    dr.xa = dt("xa", [TC, D], F32).ap()
    dr.xb = dt("xb", [TC, D], F32).ap()
    dr.moT = dt("moT", [256, TC], BF16).ap()
    dr.mixc = dt("mixc", [D, TC], BF16).ap()
    dr.hsend = dt("hsend", [NST * 128, D], F32).ap()
    dr.hgath = dt("hgath", [128 + 4 * 1024, D], F32).ap()
    dr.b_hgath = [Buf("hg%d" % i) for i in range(4)]
    dr.qkv_send, dr.qkv_gath, dr.o_send, dr.o_gath, dr.b_qkv, dr.b_o = {}, {}, {}, {}, {}, {}
    for l in ATT_LAYERS:
        dr.qkv_send[l] = [dt("snd%d_%d" % (l, i), [NST * 1024, 512], BF16).ap() for i in range(3)]
        dr.qkv_gath[l] = [[dt("gth%d_%d_%d" % (l, i, st), [4096, 512], BF16).ap() for st in range(NST)]
                          for i in range(3)]
        dr.o_send[l] = dt("osnd%d" % l, [NST * 256, 2048], BF16).ap()
        dr.o_gath[l] = [dt("ogth%d_%d" % (l, st), [1024, 2048], BF16).ap() for st in range(NST)]
        dr.b_qkv[l] = [[Buf("g") for _ in range(NST)] for _ in range(3)]
        dr.b_o[l] = [Buf("og") for _ in range(NST)]
    p = Prog(nc)
    zt = p.sb("zt", [128, D], F32); bzt = Buf("zt")
    p.op('pool', lambda e: e.memset(zt[:], 0.0), [], [bzt])
    p.dma('sp', dr.hgath[0:128, :], zt[:], bzt, reads=[bzt])
    st_cast(p, dr)
    xs = [dr.x, dr.xa, dr.xb, dr.xa, dr.y]
    for l in range(DEPTH):
        st_pre(p, dr, l, xs[l])
        if l % 3 == 0:
            st_sb(p, dr, l)
        elif l % 3 == 1:
            st_diff(p, dr, l)
        st_post(p, dr, l, xs[l], xs[l + 1])
    p.finish()
    return nc, p


def _lay(w):
    K = w.shape[0] // 128
    C = w.shape[1]
    return w.reshape(K, 128, C).transpose(1, 0, 2).reshape(128, K * C)


def _lay_fi(w):
    a = w.reshape(8, 128, 2, NJ, 128)
    return a.transpose(1, 3, 0, 2, 4).reshape(128, NJ * 2048)


_NC = []


def kernel(x, mem, g_mix, w_in, w_mem_kv, w_o, g_ffn, w_ffn_in, w_ffn_out,
           lam_q1, lam_k1, lam_q2, lam_k2, g_diff_head, conv_w, g_mem, g_final):
    f32 = np.float32
    x = np.asarray(x, f32); mem = np.asarray(mem, f32)
    g_mix = np.asarray(g_mix, f32); g_ffn = np.asarray(g_ffn, f32)
    g_mem = np.asarray(g_mem, f32); g_final = np.asarray(g_final, f32)
    w_in = np.asarray(w_in, f32); w_mem_kv = np.asarray(w_mem_kv, f32); w_o = np.asarray(w_o, f32)
    w_ffn_in = np.asarray(w_ffn_in, f32); w_ffn_out = np.asarray(w_ffn_out, f32)
    conv_w = np.asarray(conv_w, f32); g_diff_head = np.asarray(g_diff_head, f32)
    lamv = np.concatenate([np.asarray(a, f32)[0] for a in (lam_q1, lam_k1, lam_q2, lam_k2)])
    slab = np.empty((128, DEPTH * LW), f32)
    for l in range(DEPTH):
        o = l * LW
        slab[:, o + WOFF[0]:o + WOFF[1]] = _lay(w_in[l])
        slab[:, o + WOFF[1]:o + WOFF[2]] = _lay(w_mem_kv[l])
        slab[:, o + WOFF[2]:o + WOFF[3]] = _lay(w_o[l])
        slab[:, o + WOFF[3]:o + WOFF[4]] = _lay_fi(w_ffn_in[l])
        slab[:, o + WOFF[4]:o + WOFF[5]] = _lay(w_ffn_out[l])
    gv = np.concatenate([g_mix[0], g_mix[1], g_mix[2], g_mix[3], g_ffn[0], g_ffn[1], g_ffn[2], g_ffn[3],
                         g_mem, g_final])
    gvec = np.ascontiguousarray(np.tile(gv[None, :], (128, 1)))
    k = np.arange(128)[:, None]
    tri = (k >= np.arange(128)[None, :]).astype(f32).astype(NPBF)
    msk = (k < np.arange(896)[None, :] - 384).astype(f32)
    ident = np.eye(128, dtype=f32).astype(NPBF)
    ones = np.ones((128, 128), NPBF)
    cw = np.ascontiguousarray(conv_w[0].T.reshape(8, 128, 3).transpose(1, 0, 2).reshape(128, 24))
    lam_t = np.ascontiguousarray(np.tile(lamv.reshape(1, 512), (128, 1)))
    gh_t = np.ascontiguousarray(np.tile(g_diff_head[0], (128, 1)))
    if not _NC:
        _NC.append(build_fused()[0])
    nc = _NC[0]
    ims = []
    for c in range(NCORES):
        b, r = c // 4, c % 4
        xc = np.ascontiguousarray(x[b].reshape(NST, 4, 512, D)[:, r].reshape(TC, D))
        slope = 2.0 ** (-8.0 * (r + 1) / 4.0)
        dd = (np.arange(1024)[None, :] - 384 - k).astype(np.float64)
        tab = np.where(dd >= 0, -slope * dd, NEG).astype(f32)
        bias = np.tile((-slope * 128.0 * np.arange(SEQ // 128 + 1))[None, :], (128, 1)).astype(f32)
        hsel = np.zeros((128, 4), f32)
        hsel[:, (r + 3) % 4] = 1.0
        ims.append({"hsel": hsel, "x": xc, "wslab": slab, "mem": np.ascontiguousarray(mem[b]), "gvec": gvec, "ident": ident,
                    "tri": tri, "ones": ones, "msk": msk, "tab": tab, "biasT": bias, "lamv": lam_t, "gh": gh_t,
                    "cw": cw})
    res = run_bass_kernel_spmd(nc, ims, core_ids=list(range(NCORES)))
    out = np.empty((BATCH, SEQ, D), f32)
    for c in range(NCORES):
        b, r = c // 4, c % 4
        out[b].reshape(NST, 4, 512, D)[:, r] = np.asarray(res.results[c]["y"]).reshape(NST, 512, D)
    return out
```
